# Optimizing a Trainium2 kernel written in Bass

```python
import jax, jax.numpy as jnp
from jax import lax
import numpy as np

D_MODEL = 2048
BATCH = 4
SEQ = 2048
DEPTH = 4
DEC_BATCH = 16
DEC_SEQ = 2048
PAST_LEN = 128

GRID_W = 64
HEAD_DIM = 128
N_HEADS = D_MODEL // HEAD_DIM
N_KV_HEADS = N_HEADS // 4
Q_BLOCK = 128
ROPE_BASE = 10000.0
SG_WIDTH = D_MODEL
SG_GROUPS = 8
SG_CHUNK = 128
N_MEM = 256
X_HEADS = 4
X_HEAD_DIM = D_MODEL // X_HEADS
D_FF = 5632
CONV_W = 3
EPS = 1e-6

Q_COLS = N_HEADS * HEAD_DIM
KV_COLS = N_KV_HEADS * HEAD_DIM
N_IN = Q_COLS + 2 * KV_COLS + 2 * SG_WIDTH + 2 * D_MODEL

kernel_name = "hybrid_gqa_sgu_convffn_encoder"


def rms_norm(x, g):
    xf = x.astype(jnp.float32)
    y = xf * lax.rsqrt(jnp.mean(xf * xf, axis=-1, keepdims=True) + EPS)
    return (y * g.astype(jnp.float32)).astype(x.dtype)


def layer_norm(x, g, b):
    xf = x.astype(jnp.float32)
    mu = jnp.mean(xf, axis=-1, keepdims=True)
    xc = xf - mu
    y = xc * lax.rsqrt(jnp.mean(xc * xc, axis=-1, keepdims=True) + EPS)
    return (y * g.astype(jnp.float32) + b.astype(jnp.float32)).astype(x.dtype)


def axial_rope_tables(seq_len, dtype):
    rows = seq_len // GRID_W
    t = jnp.arange(rows * GRID_W)
    row = (t // GRID_W).astype(jnp.float32)
    col = (t % GRID_W).astype(jnp.float32)
    quarter = HEAD_DIM // 4
    inv = ROPE_BASE ** (-jnp.arange(quarter, dtype=jnp.float32) / quarter)
    ang_r = row[:, None] * inv[None, :]
    ang_c = col[:, None] * inv[None, :]
    ang = jnp.concatenate([ang_r, ang_r, ang_c, ang_c], axis=-1)
    return jnp.cos(ang).astype(dtype), jnp.sin(ang).astype(dtype)


def apply_rope(x, cos, sin):
    quarter = HEAD_DIM // 4
    x4 = x.reshape(x.shape[:-1] + (2, 2, quarter))
    rot = jnp.stack([-x4[..., 1, :], x4[..., 0, :]], axis=-2).reshape(x.shape)
    return x * cos[None, :, None, :] + rot * sin[None, :, None, :]


def grid_attention(q, k, v):
    B, S = q.shape[0], q.shape[1]
    nb = S // Q_BLOCK
    grp = N_HEADS // N_KV_HEADS
    qb = q.reshape(B, nb, Q_BLOCK, N_KV_HEADS, grp, HEAD_DIM).transpose(1, 0, 2, 3, 4, 5)
    scale = HEAD_DIM ** -0.5

    def block(qi):
        s = jnp.einsum('bqkgd,bskd->bkgqs', qi, k).astype(jnp.float32) * scale
        p = jax.nn.softmax(s, axis=-1).astype(v.dtype)
        return jnp.einsum('bkgqs,bskd->bqkgd', p, v)

    o = lax.map(block, qb)
    return o.transpose(1, 0, 2, 3, 4, 5).reshape(B, S, N_HEADS * HEAD_DIM)


def spatial_gating(z, ln_g, ln_b, w_s, b_s):
    B, S = z.shape[0], z.shape[1]
    u, v = jnp.split(z, 2, axis=-1)
    v = layer_norm(v, ln_g, ln_b)
    nc = S // SG_CHUNK
    cg = SG_WIDTH // SG_GROUPS
    vc = v.reshape(B, nc, SG_CHUNK, SG_GROUPS, cg)
    mixed = jnp.einsum('gpq,bnqgc->bnpgc', w_s, vc) + b_s.T[:, :, None]
    return u * mixed.reshape(B, S, SG_WIDTH)


def memory_attention(h, mem, w_xq, w_xkv, w_xo):
    B, S = h.shape[0], h.shape[1]
    n_mem = mem.shape[1]
    q = (h @ w_xq).reshape(B, S, X_HEADS, X_HEAD_DIM)
    k, v = jnp.split(mem @ w_xkv, 2, axis=-1)
    k = k.reshape(B, n_mem, X_HEADS, X_HEAD_DIM)
    v = v.reshape(B, n_mem, X_HEADS, X_HEAD_DIM)
    s = jnp.einsum('bshd,bnhd->bhsn', q, k).astype(jnp.float32) * (X_HEAD_DIM ** -0.5)
    p = jax.nn.softmax(s, axis=-1).astype(v.dtype)
    o = jnp.einsum('bhsn,bnhd->bshd', p, v).reshape(B, S, D_MODEL)
    return o @ w_xo


def depthwise_conv(a, w, b):
    S = a.shape[1]
    pad = CONV_W // 2
    ap = jnp.pad(a, ((0, 0), (pad, CONV_W - 1 - pad), (0, 0)))
    out = b
    for j in range(CONV_W):
        out = out + ap[:, j:j + S] * w[j]
    return out


def trunk(x, mem, params):
    (norm_mix_g, w_in, q_norm_g, k_norm_g, w_attn_o, sg_norm_g, sg_norm_b,
     w_spatial, b_spatial, w_sg_o, w_out, norm_x_g, norm_mem_g, w_xq, w_xkv,
     w_xo, norm_ffn_g, w_ffn_up, conv_w, conv_b, w_ffn_down, final_norm_g) = params
    B, S = x.shape[0], x.shape[1]
    cos, sin = axial_rope_tables(S, x.dtype)
    splits = [Q_COLS, Q_COLS + KV_COLS, Q_COLS + 2 * KV_COLS, Q_COLS + 2 * KV_COLS + 2 * SG_WIDTH]
    for l in range(DEPTH):
        h = rms_norm(x, norm_mix_g[l])
        q, k, v, z, gates = jnp.split(h @ w_in[l], splits, axis=-1)
        q = apply_rope(rms_norm(q.reshape(B, S, N_HEADS, HEAD_DIM), q_norm_g[l]), cos, sin)
        k = apply_rope(rms_norm(k.reshape(B, S, N_KV_HEADS, HEAD_DIM), k_norm_g[l]), cos, sin)
        v = v.reshape(B, S, N_KV_HEADS, HEAD_DIM)
        branch_a = grid_attention(q, k, v) @ w_attn_o[l]
        branch_s = spatial_gating(jax.nn.gelu(z), sg_norm_g[l], sg_norm_b[l],
                                  w_spatial[l], b_spatial[l]) @ w_sg_o[l]
        g_a, g_s = jnp.split(jax.nn.sigmoid(gates), 2, axis=-1)
        x = x + (g_a * branch_a + g_s * branch_s) @ w_out[l]
        x = x + memory_attention(rms_norm(x, norm_x_g[l]), rms_norm(mem, norm_mem_g[l]),
                                 w_xq[l], w_xkv[l], w_xo[l])
        h = rms_norm(x, norm_ffn_g[l])
        a, b = jnp.split(h @ w_ffn_up[l], 2, axis=-1)
        a = depthwise_conv(a, conv_w[l], conv_b[l])
        x = x + (jax.nn.gelu(a) * b) @ w_ffn_down[l]
    return rms_norm(x, final_norm_g)


def setup_inputs(seed: int = 0) -> dict:
    key = jax.random.key(seed)
    ks = jax.random.split(key, 32)

    def nrm(k, shape, scale):
        return jax.random.normal(k, shape, dtype=jnp.float32) * scale

    def gain(k, shape):
        return 1.0 + nrm(k, shape, 0.02)

    return {
        "x_prompt": nrm(ks[0], (BATCH, SEQ, D_MODEL), 1.0),
        "x_sample": nrm(ks[1], (DEC_BATCH, DEC_SEQ, D_MODEL), 1.0),
        "mem_prompt": nrm(ks[2], (BATCH, N_MEM, D_MODEL), 1.0),
        "mem_sample": nrm(ks[3], (DEC_BATCH, N_MEM, D_MODEL), 1.0),
        "norm_mix_g": gain(ks[4], (DEPTH, D_MODEL)),
        "w_in": nrm(ks[5], (DEPTH, D_MODEL, N_IN), D_MODEL ** -0.5),
        "q_norm_g": gain(ks[6], (DEPTH, HEAD_DIM)),
        "k_norm_g": gain(ks[7], (DEPTH, HEAD_DIM)),
        "w_attn_o": nrm(ks[8], (DEPTH, Q_COLS, D_MODEL), Q_COLS ** -0.5),
        "sg_norm_g": gain(ks[9], (DEPTH, SG_WIDTH)),
        "sg_norm_b": nrm(ks[10], (DEPTH, SG_WIDTH), 0.02),
        "w_spatial": nrm(ks[11], (DEPTH, SG_GROUPS, SG_CHUNK, SG_CHUNK), 0.5 * SG_CHUNK ** -0.5),
        "b_spatial": 1.0 + nrm(ks[12], (DEPTH, SG_GROUPS, SG_CHUNK), 0.02),
        "w_sg_o": nrm(ks[13], (DEPTH, SG_WIDTH, D_MODEL), SG_WIDTH ** -0.5),
        "w_out": nrm(ks[14], (DEPTH, D_MODEL, D_MODEL), D_MODEL ** -0.5),
        "norm_x_g": gain(ks[15], (DEPTH, D_MODEL)),
        "norm_mem_g": gain(ks[16], (DEPTH, D_MODEL)),
        "w_xq": nrm(ks[17], (DEPTH, D_MODEL, D_MODEL), D_MODEL ** -0.5),
        "w_xkv": nrm(ks[18], (DEPTH, D_MODEL, 2 * D_MODEL), D_MODEL ** -0.5),
        "w_xo": nrm(ks[19], (DEPTH, D_MODEL, D_MODEL), D_MODEL ** -0.5),
        "norm_ffn_g": gain(ks[20], (DEPTH, D_MODEL)),
        "w_ffn_up": nrm(ks[21], (DEPTH, D_MODEL, 2 * D_FF), D_MODEL ** -0.5),
        "conv_w": nrm(ks[22], (DEPTH, CONV_W, D_FF), CONV_W ** -0.5),
        "conv_b": nrm(ks[23], (DEPTH, D_FF), 0.02),
        "w_ffn_down": nrm(ks[24], (DEPTH, D_FF, D_MODEL), D_FF ** -0.5),
        "final_norm_g": gain(ks[25], (D_MODEL,)),
    }


def reference(x_prompt, x_sample, mem_prompt, mem_sample, norm_mix_g, w_in, q_norm_g,
              k_norm_g, w_attn_o, sg_norm_g, sg_norm_b, w_spatial, b_spatial, w_sg_o,
              w_out, norm_x_g, norm_mem_g, w_xq, w_xkv, w_xo, norm_ffn_g, w_ffn_up,
              conv_w, conv_b, w_ffn_down, final_norm_g):
    params = (norm_mix_g, w_in, q_norm_g, k_norm_g, w_attn_o, sg_norm_g, sg_norm_b,
              w_spatial, b_spatial, w_sg_o, w_out, norm_x_g, norm_mem_g, w_xq, w_xkv,
              w_xo, norm_ffn_g, w_ffn_up, conv_w, conv_b, w_ffn_down, final_norm_g)
    y_prompt = trunk(x_prompt, mem_prompt, params)
    y_sample = trunk(x_sample, mem_sample, params)
    return (y_prompt, y_sample)
```

```python
import numpy as np
from contextlib import ExitStack
import concourse.bass as bass
import concourse.mybir as mybir
from concourse.bass_utils import run_bass_kernel_spmd

F32 = mybir.dt.float32
BF16 = mybir.dt.bfloat16
AF = mybir.ActivationFunctionType
ALU = mybir.AluOpType

D = 2048
S = 2048
NL = 4
HD = 128
NH = 16
NKV = 4
DFF = 5632
NMEM = 256
NIN = 11264
TT = 512
NT = S // TT
NCH = D // 128
NF = DFF // 128
EPS = 1e-6
NCORES = 8
SEQ_PER_CORE = 3

EPOCH = 16000
SAME_SYNC = True
ENGS = ("pe", "act", "dve", "pool", "sp")


class Buf:
    __slots__ = ("name", "w", "r", "dsem", "dcnt", "al")

    def __init__(self, name, dsem=None):
        self.name = name
        self.w = None
        self.r = {}
        self.dsem = dsem
        self.dcnt = 0
        self.al = []


class DSem:
    __slots__ = ("sem",)

    def __init__(self, sem):
        self.sem = sem


DMA_SEM_MAX = 1800


class Sched:
    def __init__(self, nc, stack):
        self.nc = nc
        self.stack = stack
        self.lists = {e: [] for e in ENGS}
        self.count = {e: 0 for e in ENGS}
        self.sems = {e: [] for e in ENGS}
        self.known = {e: {} for e in ENGS}
        self.nsem = 0
        self.all_sems = []
        self.all_bufs = []

    def new_sem(self, name):
        self.nsem += 1
        sem = self.stack.enter_context(self.nc.semaphore(name))
        self.all_sems.append(sem)
        return sem

    def buf(self, name, dma=False):
        b = Buf(name, DSem(self.new_sem("d_" + name)) if dma else None)
        self.all_bufs.append(b)
        return b

    def reset(self):
        self.lists = {e: [] for e in ENGS}
        self.count = {e: 0 for e in ENGS}
        self.known = {e: {} for e in ENGS}
        for b in self.all_bufs:
            b.w = None
            b.r = {}
            b.dcnt = 0

    def clear_sems(self):
        for sm in self.all_sems:
            self.nc.sync.sem_clear(sm)
        self.nc.all_engine_barrier()

    def _sem_for(self, eng, idx):
        ep = idx // EPOCH
        while len(self.sems[eng]) <= ep:
            self.sems[eng].append(self.new_sem("c_%s_%d" % (eng, len(self.sems[eng]))))
        return self.sems[eng][ep], idx % EPOCH + 1

    def _waits(self, eng, reads, writes, extra):
        deps = []
        for b in reads:
            if b.w is not None:
                deps.append(b.w)
            for a in b.al:
                if a.w is not None:
                    deps.append(a.w)
        for b in writes:
            if b.w is not None:
                deps.append(b.w)
            deps.extend(b.r.values())
            for a in b.al:
                if a.w is not None:
                    deps.append(a.w)
                deps.extend(a.r.values())
        deps.extend(extra)
        cmax = {}
        dmax = {}
        for ev in deps:
            if ev[0] == "c":
                _, e2, idx = ev
                if e2 == eng and (eng in ("pe", "sp") or not SAME_SYNC):
                    continue
                if idx > cmax.get(e2, -1):
                    cmax[e2] = idx
            else:
                _, ds, cnt = ev
                if cnt > dmax.get(ds, 0):
                    dmax[ds] = cnt
        out = []
        kn = self.known[eng]
        for e2, idx in cmax.items():
            if kn.get(e2, -1) >= idx:
                continue
            kn[e2] = idx
            out.append(self._sem_for(e2, idx))
        for ds, cnt in dmax.items():
            if kn.get(ds, 0) >= cnt:
                continue
            kn[ds] = cnt
            out.append((ds.sem, cnt * 16))
        return out

    def op(self, eng, body, reads=(), writes=(), extra=()):
        waits = self._waits(eng, reads, writes, extra)
        idx = self.count[eng]
        self.count[eng] = idx + 1
        sem, _ = self._sem_for(eng, idx)
        ev = ("c", eng, idx)

        def emit(e, waits=waits, body=body, sem=sem):
            for sm, v in waits:
                e.wait_ge(sm, v)
            body(e).then_inc(sem, 1)

        self.lists[eng].append(emit)
        for b in reads:
            b.r[eng] = ev
        for b in writes:
            b.w = ev
            b.r = {}
        return ev

    def dma(self, q, out_ap, in_ap, owner, reads=(), writes=(), extra=()):
        waits = self._waits(q, reads, writes, extra)
        if owner.dcnt >= DMA_SEM_MAX:
            owner.dsem = DSem(self.new_sem("d_" + owner.name))
            owner.dcnt = 0
        owner.dcnt += 1
        ev = ("d", owner.dsem, owner.dcnt)
        sem = owner.dsem.sem

        def emit(e, waits=waits, sem=sem, out_ap=out_ap, in_ap=in_ap):
            for sm, v in waits:
                e.wait_ge(sm, v)
            e.dma_start(out=out_ap, in_=in_ap).then_inc(sem, 16)

        self.lists[q].append(emit)
        key = ("dq", owner)
        for b in reads:
            b.r[key] = ev
        for b in writes:
            b.w = ev
            b.r = {}
        return ev

    def final_wait(self, eng, events):
        waits = self._waits(eng, (), (), events)

        def emit(e, waits=waits):
            for sm, v in waits:
                e.wait_ge(sm, v)

        self.lists[eng].append(emit)

    def run(self):
        nc = self.nc
        with nc.Block() as block:
            @block.tensor
            def _(e):
                for f in self.lists["pe"]:
                    f(e)

            @block.scalar
            def _(e):
                for f in self.lists["act"]:
                    f(e)

            @block.vector
            def _(e):
                for f in self.lists["dve"]:
                    f(e)

            @block.gpsimd
            def _(e):
                for f in self.lists["pool"]:
                    f(e)

            @block.sync
            def _(e):
                for f in self.lists["sp"]:
                    f(e)


O_GMIX = 0
O_GX = O_GMIX + NL * NCH
O_GMEM = O_GX + NL * NCH
O_GFFN = O_GMEM + NL * NCH
O_SGG = O_GFFN + NL * NCH
O_SGB = O_SGG + NL * NCH
O_QG = O_SGB + NL * NCH
O_KG = O_QG + NL
O_CW = O_KG + NL
O_CB = O_CW + NL * 3 * NF
NCOL = O_CB + NL * NF

B_Q, B_K, B_V, B_U, B_SV, B_GA, B_GS = 0, 16, 20, 24, 40, 56, 72
ARENA_BYTES = 206 * 1024
RING = 10


class StopBuild(Exception):
    pass


class MK:
    def __init__(self, nc, st, nseq, nlayer):
        self.nc, self.st, self.NSEQ, self.NLY = nc, st, nseq, nlayer
        self.sc = Sched(nc, st)
        sc = self.sc
        dt = nc.dram_tensor
        I = lambda name, shape: dt(name, shape, F32, kind="ExternalInput").ap()
        self.x_in = I("x", [nseq, S, D])
        self.mem_in = I("mem", [nseq, NMEM, D])
        self.w = {}
        for name, k, n in (("w_in", D, NIN), ("w_attn_o", D, D), ("w_sg_o", D, D), ("w_out", D, D),
                           ("w_xq", D, D), ("w_xkv", D, 2 * D), ("w_xo", D, D), ("w_ffn_up", D, 2 * DFF),
                           ("w_ffn_down", DFF, D)):
            self.w[name] = I(name, [NL, k, n])
        self.colp_in = I("colp", [128, NCOL])
        self.wst_in = I("wst", [NL, 128, 1024])
        self.bsp_in = I("bsp", [NL, 1, 1024])
        self.fng_in = I("fng", [1, D])
        self.cos_in = I("cosT", [128, S])
        self.sin_in = I("sinP", [128, S])
        self.ident_in = I("ident", [128, 128])
        self.y_out = dt("y", [nseq, S, D], F32, kind="ExternalOutput").ap()
        Sc = lambda name, shape, d=BF16: dt(name, shape, d).ap()
        self.ws = {n: Sc("s_" + n, [nlayer, nb, 128, 2048]) for n, nb in
                   (("w_in", 88), ("w_attn_o", 16), ("w_sg_o", 16), ("w_xq", 16), ("w_xkv", 32), ("w_ffn_up", 88))}
        self.wm = {n: Sc("m_" + n, [nlayer, 4, ng, 128, 2048]) for n, ng in
                   (("w_sv", 4), ("w_out", 4), ("w_xo", 4), ("w_ffn_down", 11))}
        self.xa = Sc("xa", [1, S, D], F32)
        self.xb = Sc("xb", [1, S, D], F32)
        self.memc = Sc("memc", [NMEM, D], F32)
        self.b_memc = sc.buf("memc")
        self.kts = Sc("kts", [NKV, 128, S])
        self.vs = Sc("vs", [NKV, 128, NCH, 128])
        self.kms = Sc("kms", [4, 128, 1024])
        self.vms = Sc("vms", [4, 128, 1024])
        self.b_xa = [[sc.buf("xa%d_%d" % (q, t)) for t in range(NT)] for q in range(nseq)]
        self.b_xb = [[sc.buf("xb%d_%d" % (q, t)) for t in range(NT)] for q in range(nseq)]
        self.b_kts = [sc.buf("kts%d" % g) for g in range(NKV)]
        self.b_vs = [sc.buf("vs%d" % g) for g in range(NKV)]
        self.b_kms = [sc.buf("kms%d" % g) for g in range(4)]
        self.b_vms = [sc.buf("vms%d" % g) for g in range(4)]
        self.arena = st.enter_context(nc.sbuf_tensor("arena", [128, ARENA_BYTES // 2], BF16))
        self.aoff = 0
        self.ps = [st.enter_context(nc.psum_tensor("ps%d" % i, [128, 512], F32)) for i in range(8)]
        self.psbf = [p.bitcast(BF16) for p in self.ps]
        self.b_ps = [sc.buf("ps%d" % i) for i in range(8)]
        self.dma_bufs = []
        self.prologue_events = []
        self.debug = False
        self.dbg_out = {}

    def alloc(self, nbytes):
        off = self.aoff
        self.aoff += (nbytes + 63) // 64 * 64
        assert self.aoff <= ARENA_BYTES, self.aoff
        return off

    def vf(self, off, n):
        return self.arena[:, off // 2: off // 2 + 2 * n].bitcast(F32)

    def vb(self, off, n):
        return self.arena[:, off // 2: off // 2 + n]

    def dbuf(self, name):
        b = self.sc.buf(name, dma=True)
        self.dma_bufs.append(b)
        return b

    def dbg(self, name, ap, rbufs, dtype=None):
        if not getattr(self, "debug", False) or name in self.dbg_out:
            return
        n = ap.shape[1] if len(ap.shape) == 2 else int(np.prod(ap.shape[1:]))
        dtp = dtype or ap.tensor.dtype
        t = self.nc.dram_tensor("dbg_" + name, [128] + list(ap.shape[1:]), dtp, kind="ExternalOutput").ap()
        b = self.dbuf("dbg_" + name)
        self.dbg_out[name] = self.sc.dma("sp", t, ap, b, reads=rbufs)
        if name == getattr(self, "stop", None):
            raise StopBuild()

    def barrier(self):
        sc = self.sc
        evs = [("c", e, sc.count[e] - 1) for e in ENGS if sc.count[e] > 0]
        evs += [("d", b.dsem, b.dcnt) for b in self.dma_bufs if b.dcnt > 0]
        for e in ENGS:
            sc.final_wait(e, evs)

    def prologue(self):
        sc = self.sc
        save = self.aoff
        st32 = [self.alloc(32768) for _ in range(2)]
        st16 = [self.alloc(16384) for _ in range(3)]
        b32 = [self.dbuf("st32_%d" % i) for i in range(2)]
        b16 = [self.dbuf("st16_%d" % i) for i in range(3)]
        cnt = [0]
        cast_engs = ("dve", "act", "pool")

        def unit(src_ap, nch, dst_ap, blocked):
            i = cnt[0]
            cnt[0] += 1
            a32, a16 = st32[i % 2], st16[i % 3]
            n = nch * 512
            v32 = self.vf(a32, n)
            v16 = self.vb(a16, n)
            sc.dma("sp", v32.rearrange("p (c n) -> p c n", c=nch), src_ap, b32[i % 2], writes=[b32[i % 2]])
            if blocked:
                o = v16.rearrange("p (m c n) -> p c m n", m=4, c=nch, n=128)
                iv = v32.rearrange("p (c m n) -> p c m n", c=nch, m=4, n=128)
            else:
                o, iv = v16, v32
            eng = cast_engs[i % 3]
            if eng == "act":
                sc.op("act", lambda e, o=o, iv=iv: e.activation(out=o, in_=iv, func=AF.Copy),
                      reads=[b32[i % 2]], writes=[b16[i % 3]])
            else:
                sc.op(eng, lambda e, o=o, iv=iv: e.tensor_copy(out=o, in_=iv),
                      reads=[b32[i % 2]], writes=[b16[i % 3]])
            ev = sc.dma("sp", dst_ap, v16.rearrange("p (g f) -> p g f", f=2048), b16[i % 3], reads=[b16[i % 3]])
            self.prologue_events.append(ev)

        for l in range(self.NLY):
            for name in ("w_in", "w_attn_o", "w_sg_o", "w_xq", "w_xkv", "w_ffn_up"):
                W = self.w[name][l]
                ncols = W.shape[1]
                for j in range(ncols // 512):
                    src = W[:, j * 512:(j + 1) * 512].rearrange("(c p) n -> p c n", p=128)
                    dst = self.ws[name][l, j * 4:(j + 1) * 4].rearrange("m p f -> p m f")
                    unit(src, 16, dst, True)
            for name, srcname, coff in (("w_sv", "w_in", B_SV * 128), ("w_out", "w_out", 0), ("w_xo", "w_xo", 0)):
                W = self.w[srcname][l]
                for j in range(4):
                    src = W[:, coff + j * 512: coff + (j + 1) * 512].rearrange("(c p) n -> p c n", p=128)
                    dst = self.wm[name][l, j, 0:4].rearrange("g p f -> p g f")
                    unit(src, 16, dst, False)
            W = self.w["w_ffn_down"][l]
            for j in range(4):
                for g0, ng in ((0, 4), (4, 4), (8, 3)):
                    src = W[g0 * 512:(g0 + ng) * 512, j * 512:(j + 1) * 512].rearrange("(c p) n -> p c n", p=128)
                    dst = self.wm["w_ffn_down"][l, j, g0:g0 + ng].rearrange("g p f -> p g f")
                    unit(src, ng * 4, dst, False)
        self.barrier()
        self.aoff = save

    def setup_main(self):
        sc = self.sc
        A = self.alloc
        self.COLP = self.vf(A(NCOL * 4), NCOL); self.b_colp = self.dbuf("colp")
        self.IDENT = self.vb(A(256), 128); self.b_ident = sc.buf("ident")
        self.ONES = self.vb(A(256), 128); self.b_ones = sc.buf("ones")
        self.COS = self.vf(A(2048), 512); self.b_cos = self.dbuf("cos")
        self.SIN = self.vf(A(2048), 512); self.b_sin = self.dbuf("sin")
        self.WST = self.vb(A(2048), 1024); self.b_wst = sc.buf("wst")
        self.BSB = self.vf(A(4096), 1024); self.b_bsb = self.dbuf("bsb")
        self.RSS = self.vf(A(4096), 1024); self.b_rss = sc.buf("rss")
        self.XT = self.vf(A(32768), 8192); self.b_xt = self.dbuf("xt")
        self.XT3 = self.XT.rearrange("p (s n) -> p s n", s=4)
        self.HALO = self.vf(A(8192), 2048); self.b_halo = self.dbuf("halo")
        self.HB = self.vb(A(4096), 2048); self.b_hb = sc.buf("hb")
        self.HT = self.vb(A(16384), 8192); self.b_ht = sc.buf("ht")
        self.HT3 = self.HT.rearrange("p (c n) -> p c n", c=NCH)
        self.HTH = self.vb(A(64), 32); self.b_hth = sc.buf("hth")
        self.HTH3 = self.HTH.rearrange("p (c n) -> p c n", c=NCH)
        r1 = A(49152)
        self.B1 = self.vb(r1, 8192); self.b_b1 = self.dbuf("b1")
        self.B2 = self.vb(r1 + 16384, 8192); self.b_b2 = self.dbuf("b2")
        self.B3 = self.vb(r1 + 32768, 8192); self.b_b3 = self.dbuf("b3")
        self.B3_off = r1 + 32768
        self.B1_3 = self.B1.rearrange("p (c n) -> p c n", c=NCH)
        self.B2_3 = self.B2.rearrange("p (c n) -> p c n", c=NCH)
        self.GV3 = self.B3.rearrange("p (s n) -> p s n", s=4)
        self.G = self.vb(r1, NF * 512); self.b_g = sc.buf("g")
        self.G3 = self.G.rearrange("p (f n) -> p f n", f=NF)
        self.GBC = self.vf(r1 + 32768, 2048)
        self.b_gbc = self.dbuf("gbc")
        self.b_g.al = [self.b_b1, self.b_b2, self.b_b3, self.b_gbc]
        self.b_b1.al = [self.b_g]
        self.b_b2.al = [self.b_g]
        self.b_b3.al = [self.b_g, self.b_gbc]
        self.b_gbc.al = [self.b_b3, self.b_g]
        self.ring = [self.vb(A(4096), 2048) for _ in range(RING)]
        self.b_ring = [self.dbuf("ring%d" % i) for i in range(RING)]
        self.ring_i = 0
        self.f32r = [self.vf(A(2048), 512) for _ in range(8)]
        self.b_f32r = [sc.buf("f32r%d" % i) for i in range(8)]
        self.f32_i = 0
        self.bfr = [self.vb(A(1024), 512) for _ in range(6)]
        self.b_bfr = [self.dbuf("bfr%d" % i) for i in range(6)]
        self.bf_i = 0
        self.QT = [self.vb(A(1024), 512) for _ in range(2)]
        self.b_qt = [sc.buf("qt%d" % i) for i in range(2)]
        self.ABUF = [self.vf(A(2080), 514) for _ in range(2)]
        self.b_abuf = [sc.buf("abuf%d" % i) for i in range(2)]
        self.SM = self.vf(A(64), 16)
        self.b_sm = [sc.buf("sm%d" % i) for i in range(16)]
        sc.dma("sp", self.COLP, self.colp_in, self.b_colp, writes=[self.b_colp])
        tmp = self.f32r[0]
        sc.dma("sp", tmp[:, 0:128], self.ident_in, self.b_cos, writes=[self.b_f32r[0]])
        sc.op("dve", lambda e: e.tensor_copy(out=self.IDENT, in_=tmp[:, 0:128]), reads=[self.b_f32r[0]], writes=[self.b_ident])
        sc.op("pool", lambda e: e.memset(self.ONES, 1.0), writes=[self.b_ones])

    def col(self, off):
        return self.COLP[:, off:off + 1]

    def slot(self):
        i = self.ring_i
        self.ring_i = (i + 1) % RING
        return self.ring[i], self.b_ring[i]

    def f32t(self):
        i = self.f32_i
        self.f32_i = (i + 1) % 8
        return self.f32r[i], self.b_f32r[i]

    def bft(self):
        i = self.bf_i
        self.bf_i = (i + 1) % 6
        return self.bfr[i], self.b_bfr[i]

    def load_slot(self, src_ap, reads=()):
        t, b = self.slot()
        self.sc.dma("sp", t, src_ap, b, reads=reads, writes=[b])
        return t, b

    def lin_ws(self, wname, l, blk, bank, rhs3, rhs_b, ncols=512, extra_pe=None):
        sc = self.sc
        t, b = self.load_slot(self.ws[wname][l, blk])
        t3 = t.rearrange("p (c n) -> p c n", c=NCH)
        ps = self.ps[bank]

        def body(e, t3=t3, ps=ps, rhs3=rhs3, ncols=ncols):
            r = None
            for c in range(NCH):
                r = e.matmul(ps[:, 0:ncols], lhsT=t3[:, c, :], rhs=rhs3[:, c, 0:ncols], start=(c == 0), stop=(c == NCH - 1))
            return r

        sc.op("pe", body, reads=[b, rhs_b], writes=[self.b_ps[bank]])
        return t3, b

    def lin_moving_residual(self, wname, l, in3, in_b, ngroups):
        sc = self.sc
        for j in range(4):
            base = (j % 2) * 4
            for g4 in range(ngroups):
                t, b = self.load_slot(self.wm[wname][l, j, g4])
                t3 = t.rearrange("p (c n) -> p c n", c=4)
                for s_ in range(4):
                    def body(e, t3=t3, s_=s_, g4=g4, base=base):
                        r = None
                        for c in range(4):
                            r = e.matmul(self.ps[base + s_][:, :], lhsT=in3[:, g4 * 4 + c, s_ * 128:(s_ + 1) * 128],
                                         rhs=t3[:, c, :], start=(g4 == 0 and c == 0),
                                         stop=(g4 == ngroups - 1 and c == 3))
                        return r
                    sc.op("pe", body, reads=[b, in_b], writes=[self.b_ps[base + s_]])
            for s_ in range(4):
                xs = self.XT3[:, s_, j * 512:(j + 1) * 512]
                sc.op("dve", lambda e, xs=xs, p=self.ps[base + s_]: e.tensor_tensor(out=xs, in0=p[:, :], in1=xs, op=ALU.add),
                      reads=[self.b_ps[base + s_], self.b_xt], writes=[self.b_xt])

    def norm_sub(self, xap, xbuf, goff, dest, dest_b, ncols, banks=(0, 1)):
        sc = self.sc
        SS, RT, RS = self.SM[:, 0:1], self.SM[:, 1:2], self.SM[:, 2:3]
        bss, brt, brs = self.b_sm[0], self.b_sm[1], self.b_sm[2]
        sc.op("act", lambda e: e.activation(out=self.HB, in_=xap, func=AF.Square, accum_out=SS),
              reads=[xbuf], writes=[self.b_hb, bss])
        sc.op("act", lambda e: e.activation(out=RT, in_=SS, func=AF.Ln, scale=1.0 / D, bias=EPS), reads=[bss], writes=[brt])
        sc.op("act", lambda e: e.activation(out=RS, in_=RT, func=AF.Exp, scale=-0.5), reads=[brt], writes=[brs])
        sc.op("dve", lambda e: e.tensor_scalar(out=self.HB, in0=xap, scalar1=RS, scalar2=None, op0=ALU.mult),
              reads=[xbuf, brs], writes=[self.b_hb])
        for half in range(2):
            bank = banks[half]
            pb = self.psbf[bank]

            def body(e, half=half, pb=pb):
                r = None
                for k in range(8):
                    c = half * 8 + k
                    r = e.transpose(out=pb[:, k * 128:(k + 1) * 128], in_=self.HB[:, c * 128:(c + 1) * 128], identity=self.IDENT)
                return r
            sc.op("pe", body, reads=[self.b_hb, self.b_ident], writes=[self.b_ps[bank]])
            for k in range(8):
                c = half * 8 + k
                src = pb[:, k * 128:k * 128 + ncols]
                d = dest(c)
                g = self.col(goff + c)
                if k % 2 == 0:
                    sc.op("act", lambda e, d=d, src=src, g=g: e.activation(out=d, in_=src, func=AF.Copy, scale=g),
                          reads=[self.b_ps[bank], self.b_colp], writes=[dest_b])
                else:
                    sc.op("dve", lambda e, d=d, src=src, g=g: e.tensor_scalar(out=d, in0=src, scalar1=g, scalar2=None, op0=ALU.mult),
                          reads=[self.b_ps[bank], self.b_colp], writes=[dest_b])

    def norm_tile(self, goff):
        for s_ in range(4):
            self.norm_sub(self.XT3[:, s_, :], self.b_xt, goff,
                          lambda c, s_=s_: self.HT3[:, c, s_ * 128:(s_ + 1) * 128], self.b_ht, 128)

    def qk_rope(self, bank, sumbank, gcol, dest, dest_b):
        sc = self.sc
        ps, psb = self.ps[bank], self.b_ps[bank]
        sq, bsq = self.bft()
        sc.op("act", lambda e: e.activation(out=sq, in_=ps[:, :], func=AF.Square), reads=[psb], writes=[bsq])
        p2, p2b = self.ps[sumbank], self.b_ps[sumbank]
        sc.op("pe", lambda e: e.matmul(p2[:, :], lhsT=self.ONES, rhs=sq, start=True, stop=True),
              reads=[bsq, self.b_ones], writes=[p2b])
        rt, brt = self.f32t()
        sc.op("act", lambda e: e.activation(out=rt, in_=p2[:, :], func=AF.Ln, scale=1.0 / HD, bias=EPS), reads=[p2b], writes=[brt])
        sc.op("act", lambda e: e.activation(out=rt, in_=rt, func=AF.Exp, scale=-0.5), reads=[brt], writes=[brt])
        kn, bkn = self.f32t()
        sc.op("dve", lambda e: e.scalar_tensor_tensor(out=kn, in0=ps[:, :], scalar=gcol, in1=rt, op0=ALU.mult, op1=ALU.mult),
              reads=[psb, brt, self.b_colp], writes=[bkn])
        t1, bt1 = self.f32t()
        sc.op("pool", lambda e: e.tensor_tensor(out=t1, in0=kn, in1=self.COS, op=ALU.mult), reads=[bkn, self.b_cos], writes=[bt1])
        t2, bt2 = self.f32t()
        for qd in range(4):
            pq = qd ^ 1
            sc.op("pool", lambda e, qd=qd, pq=pq: e.tensor_tensor(out=t2[qd * 32:(qd + 1) * 32, :], in0=kn[pq * 32:(pq + 1) * 32, :],
                                                                 in1=self.SIN[pq * 32:(pq + 1) * 32, :], op=ALU.mult),
                  reads=[bkn, self.b_sin], writes=[bt2])
        sc.op("pool", lambda e: e.tensor_tensor(out=dest, in0=t1, in1=t2, op=ALU.add), reads=[bt1, bt2], writes=[dest_b])

    def load_x_tile(self, src_ap, src_bufs, tt):
        self.sc.dma("sp", self.XT3, src_ap[tt * TT:(tt + 1) * TT, :].rearrange("(s p) n -> p s n", p=128),
                    self.b_xt, reads=src_bufs, writes=[self.b_xt])

    def store_x_tile(self, dst_ap, dst_bufs, tt):
        return self.sc.dma("sp", dst_ap[tt * TT:(tt + 1) * TT, :].rearrange("(s p) n -> p s n", p=128), self.XT3,
                           self.b_xt, reads=[self.b_xt], writes=dst_bufs)

    def load_cs(self, tt):
        sc = self.sc
        sc.dma("sp", self.COS, self.cos_in[:, tt * TT:(tt + 1) * TT], self.b_cos, writes=[self.b_cos])
        sc.dma("sp", self.SIN, self.sin_in[:, tt * TT:(tt + 1) * TT], self.b_sin, writes=[self.b_sin])

    def layer_consts(self, l):
        sc = self.sc
        stg = self.vf(self.B3_off, 1024)
        sc.dma("sp", stg, self.wst_in[l], self.b_b3, writes=[self.b_b3])
        sc.op("dve", lambda e: e.tensor_copy(out=self.WST, in_=stg), reads=[self.b_b3], writes=[self.b_wst])
        sc.dma("sp", self.BSB, self.bsp_in[l].partition_broadcast(128), self.b_bsb, writes=[self.b_bsb])
        for h in range(2):
            sc.op("pe", lambda e, h=h: e.matmul(self.ps[h][:, :], lhsT=self.ONES, rhs=self.WST[:, h * 512:(h + 1) * 512],
                                                start=True, stop=True),
                  reads=[self.b_ones, self.b_wst], writes=[self.b_ps[h]])
            sc.op("act", lambda e, h=h: e.activation(out=self.RSS[:, h * 512:(h + 1) * 512], in_=self.ps[h][:, :], func=AF.Copy),
                  reads=[self.b_ps[h]], writes=[self.b_rss])

    def mem_kv(self, l, q):
        sc = self.sc
        sc.dma("sp", self.XT3[:, 0:2, :], self.memc.rearrange("(s p) n -> p s n", p=128), self.b_xt, reads=[self.b_memc], writes=[self.b_xt])
        for s_ in range(2):
            self.norm_sub(self.XT3[:, s_, :], self.b_xt, O_GMEM + l * NCH,
                          lambda c, s_=s_: self.HT3[:, c, s_ * 128:(s_ + 1) * 128], self.b_ht, 128)
        KST = self.B1.rearrange("p (m n) -> p m n", m=32)
        VST = self.B2.rearrange("p (k n) -> p k n", k=4)
        for m in range(NCH):
            bank = m % 2
            self.lin_ws("w_xkv", l, m, bank, self.HT3, self.b_ht, ncols=256)
            sc.op("act", lambda e, m=m, bank=bank: e.activation(out=KST[:, m, :], in_=self.ps[bank][:, 0:256], func=AF.Copy),
                  reads=[self.b_ps[bank]], writes=[self.b_b1])
        for hx in range(4):
            sc.dma("sp", self.kms[hx].rearrange("p (m n) -> p m n", m=4), KST[:, hx * 4:(hx + 1) * 4, :], self.b_b1,
                   reads=[self.b_b1], writes=[self.b_kms[hx]])
        for m in range(NCH):
            bank = 2 + m % 2
            self.lin_ws("w_xkv", l, NCH + m, bank, self.HT3, self.b_ht, ncols=256)
            vt, bvt = self.bft()
            sc.op("act", lambda e, bank=bank, vt=vt: e.activation(out=vt[:, 0:256], in_=self.ps[bank][:, 0:256], func=AF.Copy),
                  reads=[self.b_ps[bank]], writes=[bvt])
            tb = 4 + m % 2
            pb = self.psbf[tb]

            def body(e, vt=vt, pb=pb):
                e.transpose(out=pb[:, 0:128], in_=vt[:, 0:128], identity=self.IDENT)
                return e.transpose(out=pb[:, 128:256], in_=vt[:, 128:256], identity=self.IDENT)
            sc.op("pe", body, reads=[bvt, self.b_ident], writes=[self.b_ps[tb]])
            sc.op("dve", lambda e, m=m, pb=pb: e.tensor_copy(out=VST[:, 0:2, m * 128:(m + 1) * 128],
                                                             in_=pb[:, 0:256].rearrange("p (k n) -> p k n", k=2)),
                  reads=[self.b_ps[tb]], writes=[self.b_b2])
        for hx in range(4):
            sc.dma("sp", self.vms[hx].rearrange("p (k n) -> p k n", k=2), VST[:, 0:2, hx * 512:(hx + 1) * 512], self.b_b2,
                   reads=[self.b_b2], writes=[self.b_vms[hx]])

    def p1(self, l, q, xsrc, xsrc_b):
        sc = self.sc
        for tt in range(NT):
            self.load_x_tile(xsrc[q], [xsrc_b[q][tt]], tt)
            self.load_cs(tt)
            self.norm_tile(O_GMIX + l * NCH)
            self.dbg("p1_ht", self.HT, [self.b_ht])
            for g in range(NKV):
                bank = g % 2
                self.lin_ws("w_in", l, B_K + g, bank, self.HT3, self.b_ht)
                kt, bkt = self.bft()
                self.qk_rope(bank, 2 + g % 2, self.col(O_KG + l), kt, bkt)
                self.dbg("p1_kt", kt, [bkt])
                sc.dma("sp", self.kts[g][:, tt * TT:(tt + 1) * TT], kt, bkt, reads=[bkt], writes=[self.b_kts[g]])
            for g in range(NKV):
                bank = 4 + g % 2
                self.lin_ws("w_in", l, B_V + g, bank, self.HT3, self.b_ht)
                vt, bvt = self.bft()
                sc.op("act", lambda e, bank=bank, vt=vt: e.activation(out=vt, in_=self.ps[bank][:, :], func=AF.Copy),
                      reads=[self.b_ps[bank]], writes=[bvt])
                tb = 6 + g % 2
                pb = self.psbf[tb]

                def body(e, vt=vt, pb=pb):
                    r = None
                    for k in range(4):
                        r = e.transpose(out=pb[:, k * 128:(k + 1) * 128], in_=vt[:, k * 128:(k + 1) * 128], identity=self.IDENT)
                    return r
                sc.op("pe", body, reads=[bvt, self.b_ident], writes=[self.b_ps[tb]])
                vo, bvo = self.bft()
                sc.op("dve", lambda e, vo=vo, pb=pb: e.tensor_copy(out=vo, in_=pb[:, 0:512]), reads=[self.b_ps[tb]], writes=[bvo])
                sc.dma("sp", self.vs[g][:, tt * 4:(tt + 1) * 4, :], vo.rearrange("p (k n) -> p k n", k=4), bvo,
                       reads=[bvo], writes=[self.b_vs[g]])

    def p23(self, l, q, xsrc, xsrc_b):
        sc = self.sc
        for tt in range(NT):
            self.load_x_tile(xsrc[q], [xsrc_b[q][tt]], tt)
            self.load_cs(tt)
            self.norm_tile(O_GMIX + l * NCH)
            self.sgu(l)
            self.dbg("st", self.B1, [self.b_b1])
            self.dbg("vn", self.B3, [self.b_b3])
            self.branch(l, "w_sg_o", B_GS, self.B1_3, self.b_b1, first=True)
            self.dbg("sb", self.B2, [self.b_b2])
            self.attention(l)
            self.dbg("at", self.B1, [self.b_b1])
            self.branch(l, "w_attn_o", B_GA, self.B1_3, self.b_b1, first=False)
            self.dbg("mix", self.B2, [self.b_b2])
            self.lin_moving_residual("w_out", l, self.B2_3, self.b_b2, 4)
            self.dbg("x1", self.XT, [self.b_xt])
            self.cross(l)
            self.dbg("x2", self.XT, [self.b_xt])
            self.store_x_tile(self.xa[q], [self.b_xa[q][tt]], tt)

    def sgu(self, l):
        sc = self.sc
        UT3 = self.B1_3
        for m in range(NCH):
            bank = m % 2
            self.lin_ws("w_in", l, B_U + m, bank, self.HT3, self.b_ht)
            sc.op("act", lambda e, m=m, bank=bank: e.activation(out=UT3[:, m, :], in_=self.ps[bank][:, :], func=AF.Gelu_apprx_tanh),
                  reads=[self.b_ps[bank]], writes=[self.b_b1])
        for j in range(4):
            base = 4 if j % 2 == 0 else 0
            if j % 2 == 1:
                base = 2
            banks = (4, 5, 6, 7) if j % 2 == 0 else (2, 3, 6, 7)
            banks = (4, 5, 6, 7)
            for g4 in range(4):
                t, b = self.load_slot(self.wm["w_sv"][l, j, g4])
                t3 = t.rearrange("p (c n) -> p c n", c=4)
                for s_ in range(4):
                    def body(e, t3=t3, s_=s_, g4=g4, banks=banks):
                        r = None
                        for c in range(4):
                            r = e.matmul(self.ps[banks[s_]][:, :], lhsT=self.HT3[:, g4 * 4 + c, s_ * 128:(s_ + 1) * 128],
                                         rhs=t3[:, c, :], start=(g4 == 0 and c == 0), stop=(g4 == 3 and c == 3))
                        return r
                    sc.op("pe", body, reads=[b, self.b_ht], writes=[self.b_ps[banks[s_]]])
            for s_ in range(4):
                sc.op("act", lambda e, s_=s_, j=j, banks=banks: e.activation(out=self.GV3[:, s_, j * 512:(j + 1) * 512],
                                                                          in_=self.ps[banks[s_]][:, :], func=AF.Gelu_apprx_tanh),
                      reads=[self.b_ps[banks[s_]]], writes=[self.b_b3])
        S1, S2, MEAN, VAR, RSTD, NMR = [self.SM[:, 4 + i:5 + i] for i in range(6)]
        bs1, bs2, bmean, bvar, brstd, bnmr = [self.b_sm[4 + i] for i in range(6)]
        for s_ in range(4):
            gv = self.GV3[:, s_, :]
            sc.op("act", lambda e, gv=gv: e.activation(out=self.HB, in_=gv, func=AF.Identity, accum_out=S1),
                  reads=[self.b_b3], writes=[self.b_hb, bs1])
            sc.op("act", lambda e, gv=gv: e.activation(out=self.HB, in_=gv, func=AF.Square, accum_out=S2),
                  reads=[self.b_b3], writes=[self.b_hb, bs2])
            sc.op("dve", lambda e: e.tensor_scalar(out=MEAN, in0=S1, scalar1=1.0 / D, scalar2=None, op0=ALU.mult),
                  reads=[bs1], writes=[bmean])
            sc.op("dve", lambda e: e.tensor_tensor(out=VAR, in0=MEAN, in1=MEAN, op=ALU.mult), reads=[bmean], writes=[bvar])
            sc.op("dve", lambda e: e.scalar_tensor_tensor(out=VAR, in0=S2, scalar=1.0 / D, in1=VAR, op0=ALU.mult, op1=ALU.subtract),
                  reads=[bs2, bvar], writes=[bvar])
            sc.op("act", lambda e: e.activation(out=RSTD, in_=VAR, func=AF.Ln, bias=EPS), reads=[bvar], writes=[brstd])
            sc.op("act", lambda e: e.activation(out=RSTD, in_=RSTD, func=AF.Exp, scale=-0.5), reads=[brstd], writes=[brstd])
            sc.op("dve", lambda e: e.scalar_tensor_tensor(out=NMR, in0=MEAN, scalar=-1.0, in1=RSTD, op0=ALU.mult, op1=ALU.mult),
                  reads=[bmean, brstd], writes=[bnmr])
            sc.op("dve", lambda e, gv=gv: e.tensor_scalar(out=gv, in0=gv, scalar1=RSTD, scalar2=NMR, op0=ALU.mult, op1=ALU.add),
                  reads=[self.b_b3, brstd, bnmr], writes=[self.b_b3])
        for cb in range(NCH):
            g = cb // 2
            bank = cb % 2

            def body(e, cb=cb, g=g, bank=bank):
                r = None
                for s_ in range(4):
                    r = e.matmul(self.ps[bank][:, s_ * 128:(s_ + 1) * 128], lhsT=self.GV3[:, s_, cb * 128:(cb + 1) * 128],
                                 rhs=self.WST[:, g * 128:(g + 1) * 128], start=True, stop=True)
                return r
            sc.op("pe", body, reads=[self.b_b3, self.b_wst], writes=[self.b_ps[bank]])
            tb, btb = self.f32t()
            sc.op("dve", lambda e, tb=tb, g=g, cb=cb: e.scalar_tensor_tensor(
                out=tb[:, 0:128], in0=self.RSS[:, g * 128:(g + 1) * 128], scalar=self.col(O_SGB + l * NCH + cb),
                in1=self.BSB[:, g * 128:(g + 1) * 128], op0=ALU.mult, op1=ALU.add),
                reads=[self.b_rss, self.b_bsb, self.b_colp], writes=[btb])
            tm, btm = self.f32t()
            for s_ in range(4):
                sc.op("dve", lambda e, tm=tm, tb=tb, s_=s_, cb=cb, bank=bank: e.scalar_tensor_tensor(
                    out=tm[:, s_ * 128:(s_ + 1) * 128], in0=self.ps[bank][:, s_ * 128:(s_ + 1) * 128],
                    scalar=self.col(O_SGG + l * NCH + cb), in1=tb[:, 0:128], op0=ALU.mult, op1=ALU.add),
                    reads=[self.b_ps[bank], btb, self.b_colp], writes=[btm])
            sc.op("pool", lambda e, tm=tm, cb=cb: e.tensor_tensor(out=UT3[:, cb, :], in0=tm, in1=UT3[:, cb, :], op=ALU.mult),
                  reads=[btm, self.b_b1], writes=[self.b_b1])

    def branch(self, l, wname, gate_blk, in3, in_b, first):
        sc = self.sc
        for m in range(NCH):
            ba, bg = (0, 2) if m % 2 == 0 else (1, 3)
            self.lin_ws(wname, l, m, ba, in3, in_b)
            self.lin_ws("w_in", l, gate_blk + m, bg, self.HT3, self.b_ht)
            sg, bsg = self.f32t()
            sc.op("act", lambda e, sg=sg, bg=bg: e.activation(out=sg, in_=self.ps[bg][:, :], func=AF.Sigmoid),
                  reads=[self.b_ps[bg]], writes=[bsg])
            if first:
                sc.op("dve", lambda e, sg=sg, ba=ba, m=m: e.tensor_tensor(out=self.B2_3[:, m, :], in0=self.ps[ba][:, :], in1=sg, op=ALU.mult),
                      reads=[self.b_ps[ba], bsg], writes=[self.b_b2])
            else:
                tm, btm = self.f32t()
                sc.op("dve", lambda e, sg=sg, ba=ba, tm=tm: e.tensor_tensor(out=tm, in0=self.ps[ba][:, :], in1=sg, op=ALU.mult),
                      reads=[self.b_ps[ba], bsg], writes=[btm])
                sc.op("pool", lambda e, tm=tm, m=m: e.tensor_tensor(out=self.B2_3[:, m, :], in0=tm, in1=self.B2_3[:, m, :], op=ALU.add),
                      reads=[btm, self.b_b2], writes=[self.b_b2])

    def attention(self, l):
        sc = self.sc
        scale = float(HD) ** -0.5
        for g in range(NKV):
            ktt, bktt = self.load_slot(self.kts[g], reads=[self.b_kts[g]])
            vtt, bvtt = self.load_slot(self.vs[g].rearrange("p k n -> p (k n)"), reads=[self.b_vs[g]])
            v3 = vtt.rearrange("p (k n) -> p k n", k=NCH)
            self.dbg("ktt%d" % g, ktt, [bktt])
            self.dbg("vtt%d" % g, vtt, [bvtt])
            for hh in range(4):
                h = g * 4 + hh
                pbank = hh % 2
                self.lin_ws("w_in", l, B_Q + h, pbank, self.HT3, self.b_ht)
                qt, bqt = self.QT[hh % 2], self.b_qt[hh % 2]
                ob, sb = (4, 6) if hh % 2 == 0 else (5, 7)
                self.qk_rope(pbank, sb, self.col(O_QG + l), qt, bqt)
                self.dbg("qt%d" % h, qt, [bqt])
                pts = {}

                def s_mm(kt):
                    bank = 2 + kt % 2
                    sc.op("pe", lambda e, kt=kt, bank=bank, ktt=ktt, qt=qt: e.matmul(self.ps[bank][:, :], lhsT=ktt[:, kt * 128:(kt + 1) * 128], rhs=qt,
                                                                                     start=True, stop=True),
                          reads=[bktt, bqt], writes=[self.b_ps[bank]])
                s_mm(0)
                for kt in range(NCH):
                    if kt + 1 < NCH:
                        s_mm(kt + 1)
                    bank = 2 + kt % 2
                    pt, bpt = self.bft()
                    sc.op("act", lambda e, pt=pt, bank=bank: e.activation(out=pt, in_=self.ps[bank][:, :], func=AF.Exp, scale=scale),
                          reads=[self.b_ps[bank]], writes=[bpt])

                    def body(e, kt=kt, pt=pt, ob=ob, sb=sb, v3=v3):
                        e.matmul(self.ps[ob][:, :], lhsT=v3[:, kt, :], rhs=pt, start=(kt == 0), stop=(kt == NCH - 1))
                        return e.matmul(self.ps[sb][:, :], lhsT=self.ONES, rhs=pt, start=(kt == 0), stop=(kt == NCH - 1))
                    sc.op("pe", body, reads=[bvtt, bpt, self.b_ones], writes=[self.b_ps[ob], self.b_ps[sb]])
                rc, brc = self.f32t()
                sc.op("act", lambda e, rc=rc, sb=sb: e.activation(out=rc, in_=self.ps[sb][:, :], func=AF.Ln), reads=[self.b_ps[sb]], writes=[brc])
                sc.op("act", lambda e, rc=rc: e.activation(out=rc, in_=rc, func=AF.Exp, scale=-1.0), reads=[brc], writes=[brc])
                sc.op("dve", lambda e, rc=rc, ob=ob, h=h: e.tensor_tensor(out=self.B1_3[:, h, :], in0=self.ps[ob][:, :], in1=rc, op=ALU.mult),
                      reads=[self.b_ps[ob], brc], writes=[self.b_b1])

    def cross(self, l):
        sc = self.sc
        scale = 512.0 ** -0.5
        self.norm_tile(O_GX + l * NCH)
        QX3, OX3 = self.B1_3, self.B2_3
        for m in range(NCH):
            bank = m % 2
            self.lin_ws("w_xq", l, m, bank, self.HT3, self.b_ht)
            sc.op("act", lambda e, m=m, bank=bank: e.activation(out=QX3[:, m, :], in_=self.ps[bank][:, :], func=AF.Copy),
                  reads=[self.b_ps[bank]], writes=[self.b_b1])
        for hx in range(4):
            t, b = self.slot()
            sc.dma("sp", t[:, 0:1024], self.kms[hx], b, reads=[self.b_kms[hx]], writes=[b])
            sc.dma("sp", t[:, 1024:2048], self.vms[hx], b, reads=[self.b_vms[hx]], writes=[b])
            km3 = t[:, 0:1024].rearrange("p (m n) -> p m n", m=4)
            vm3 = t[:, 1024:2048].rearrange("p (k n) -> p k n", k=2)
            pts = []
            for kt in range(2):
                bank = kt

                def body(e, kt=kt, bank=bank, hx=hx, km3=km3):
                    r = None
                    for dc in range(4):
                        r = e.matmul(self.ps[bank][:, :], lhsT=km3[:, dc, kt * 128:(kt + 1) * 128], rhs=QX3[:, hx * 4 + dc, :],
                                     start=(dc == 0), stop=(dc == 3))
                    return r
                sc.op("pe", body, reads=[b, self.b_b1], writes=[self.b_ps[bank]])
                pt, bpt = self.bft()
                sc.op("act", lambda e, pt=pt, bank=bank: e.activation(out=pt, in_=self.ps[bank][:, :], func=AF.Exp, scale=scale),
                      reads=[self.b_ps[bank]], writes=[bpt])
                pts.append((pt, bpt))

            def body_sum(e, pts=pts):
                e.matmul(self.ps[2][:, :], lhsT=self.ONES, rhs=pts[0][0], start=True, stop=False)
                return e.matmul(self.ps[2][:, :], lhsT=self.ONES, rhs=pts[1][0], start=False, stop=True)
            sc.op("pe", body_sum, reads=[pts[0][1], pts[1][1], self.b_ones], writes=[self.b_ps[2]])
            rc, brc = self.f32t()
            sc.op("act", lambda e, rc=rc: e.activation(out=rc, in_=self.ps[2][:, :], func=AF.Ln), reads=[self.b_ps[2]], writes=[brc])
            sc.op("act", lambda e, rc=rc: e.activation(out=rc, in_=rc, func=AF.Exp, scale=-1.0), reads=[brc], writes=[brc])
            for dc in range(4):
                ob = 4 + dc

                def body_o(e, dc=dc, ob=ob, vm3=vm3, pts=pts):
                    e.matmul(self.ps[ob][:, :], lhsT=vm3[:, 0, dc * 128:(dc + 1) * 128], rhs=pts[0][0], start=True, stop=False)
                    return e.matmul(self.ps[ob][:, :], lhsT=vm3[:, 1, dc * 128:(dc + 1) * 128], rhs=pts[1][0], start=False, stop=True)
                sc.op("pe", body_o, reads=[b, pts[0][1], pts[1][1]], writes=[self.b_ps[ob]])
                sc.op("dve", lambda e, rc=rc, ob=ob, hx=hx, dc=dc: e.tensor_tensor(out=OX3[:, hx * 4 + dc, :], in0=self.ps[ob][:, :], in1=rc, op=ALU.mult),
                      reads=[self.b_ps[ob], brc], writes=[self.b_b2])
        self.dbg("qx", self.B1, [self.b_b1])
        self.dbg("ox", self.B2, [self.b_b2])
        self.lin_moving_residual("w_xo", l, OX3, self.b_b2, 4)

    def p4(self, l, q):
        sc = self.sc
        for tt in range(NT):
            self.load_x_tile(self.xa[q], [self.b_xa[q][tt]], tt)
            sc.op("pool", lambda e: e.memset(self.HALO, 0.0), writes=[self.b_halo])
            if tt > 0:
                r = tt * TT - 1
                sc.dma("sp", self.HALO[0:1, :], self.xa[q][r:r + 1, :], self.b_halo, reads=[self.b_xa[q][tt - 1]], writes=[self.b_halo])
            if tt < NT - 1:
                r = (tt + 1) * TT
                sc.dma("sp", self.HALO[1:2, :], self.xa[q][r:r + 1, :], self.b_halo, reads=[self.b_xa[q][tt + 1]], writes=[self.b_halo])
            goff = O_GFFN + l * NCH
            self.norm_tile(goff)
            self.norm_sub(self.HALO, self.b_halo, goff, lambda c: self.HTH3[:, c, :], self.b_hth, 2)
            for f in range(NF):
                pa, pb_, ph = (0, 2, 4) if f % 2 == 0 else (1, 3, 5)
                ta, ba = self.load_slot(self.ws["w_ffn_up"][l, f])
                ta3 = ta.rearrange("p (c n) -> p c n", c=NCH)

                def body(e, ta3=ta3, pa=pa, ph=ph):
                    r = None
                    for c in range(NCH):
                        e.matmul(self.ps[pa][:, :], lhsT=ta3[:, c, :], rhs=self.HT3[:, c, :], start=(c == 0), stop=(c == NCH - 1))
                        r = e.matmul(self.ps[ph][:, 0:2], lhsT=ta3[:, c, :], rhs=self.HTH3[:, c, :], start=(c == 0), stop=(c == NCH - 1))
                    return r
                sc.op("pe", body, reads=[ba, self.b_ht, self.b_hth], writes=[self.b_ps[pa], self.b_ps[ph]])
                self.lin_ws("w_ffn_up", l, NF + f, pb_, self.HT3, self.b_ht)
                ab, bab = self.ABUF[f % 2], self.b_abuf[f % 2]
                sc.op("act", lambda e, ab=ab, pa=pa: e.activation(out=ab[:, 1:513], in_=self.ps[pa][:, :], func=AF.Copy),
                      reads=[self.b_ps[pa]], writes=[bab])
                sc.op("act", lambda e, ab=ab, ph=ph: e.activation(out=ab[:, 0:1], in_=self.ps[ph][:, 0:1], func=AF.Copy),
                      reads=[self.b_ps[ph]], writes=[bab])
                sc.op("act", lambda e, ab=ab, ph=ph: e.activation(out=ab[:, 513:514], in_=self.ps[ph][:, 1:2], func=AF.Copy),
                      reads=[self.b_ps[ph]], writes=[bab])
                cw = O_CW + l * 3 * NF + f
                c1, bc1 = self.f32t()
                sc.op("pool", lambda e, c1=c1, ab=ab, cw=cw: e.tensor_scalar(out=c1, in0=ab[:, 0:512], scalar1=self.col(cw), scalar2=None, op0=ALU.mult),
                      reads=[bab, self.b_colp], writes=[bc1])
                sc.op("dve", lambda e, c1=c1, ab=ab, cw=cw: e.scalar_tensor_tensor(out=c1, in0=ab[:, 1:513], scalar=self.col(cw + NF), in1=c1,
                                                                                 op0=ALU.mult, op1=ALU.add),
                      reads=[bab, bc1, self.b_colp], writes=[bc1])
                sc.op("dve", lambda e, c1=c1, ab=ab, cw=cw: e.scalar_tensor_tensor(out=c1, in0=ab[:, 2:514], scalar=self.col(cw + 2 * NF), in1=c1,
                                                                                 op0=ALU.mult, op1=ALU.add),
                      reads=[bab, bc1, self.b_colp], writes=[bc1])
                sc.op("act", lambda e, c1=c1, f=f: e.activation(out=c1, in_=c1, func=AF.Gelu_apprx_tanh, bias=self.col(O_CB + l * NF + f)),
                      reads=[bc1, self.b_colp], writes=[bc1])
                sc.op("dve", lambda e, c1=c1, f=f, pb_=pb_: e.tensor_tensor(out=self.G3[:, f, :], in0=self.ps[pb_][:, :], in1=c1, op=ALU.mult),
                      reads=[self.b_ps[pb_], bc1], writes=[self.b_g])
            self.dbg("g", self.G, [self.b_g])
            self.lin_moving_residual("w_ffn_down", l, self.G3, self.b_g, 11)
            self.dbg("x3", self.XT, [self.b_xt])
            self.store_x_tile(self.xb[q], [self.b_xb[q][tt]], tt)

    def final(self, q, ydst):
        sc = self.sc
        sc.dma("sp", self.GBC, self.fng_in.partition_broadcast(128), self.b_gbc, writes=[self.b_gbc])
        SS, RT, RS = self.SM[:, 0:1], self.SM[:, 1:2], self.SM[:, 2:3]
        bss, brt, brs = self.b_sm[0], self.b_sm[1], self.b_sm[2]
        evs = []
        for tt in range(NT):
            self.load_x_tile(self.xb[q], [self.b_xb[q][tt]], tt)
            for s_ in range(4):
                xap = self.XT3[:, s_, :]
                sc.op("act", lambda e, xap=xap: e.activation(out=self.HB, in_=xap, func=AF.Square, accum_out=SS),
                      reads=[self.b_xt], writes=[self.b_hb, bss])
                sc.op("act", lambda e: e.activation(out=RT, in_=SS, func=AF.Ln, scale=1.0 / D, bias=EPS), reads=[bss], writes=[brt])
                sc.op("act", lambda e: e.activation(out=RS, in_=RT, func=AF.Exp, scale=-0.5), reads=[brt], writes=[brs])
                sc.op("dve", lambda e, xap=xap: e.scalar_tensor_tensor(out=xap, in0=xap, scalar=RS, in1=self.GBC, op0=ALU.mult, op1=ALU.mult),
                      reads=[self.b_xt, brs, self.b_gbc], writes=[self.b_xt])
            evs.append(self.store_x_tile(ydst, [], tt))
        return evs


def build(nseq, nlayer, do_prologue=True, debug=False, stop=None):
    nc = bass.Bass("TRN2", target_bir_lowering=False)
    st = ExitStack()
    mk = MK(nc, st, nseq, nlayer)
    mk.debug = debug
    mk.stop = stop
    sc = mk.sc
    if do_prologue:
        mk.prologue()
        sc.run()
        sc.clear_sems()
        sc.reset()
    with nc.Fori(0, nseq) as qv:
        mk.setup_main()
        b_cp = mk.dbuf("cpin")
        xsrc = mk.x_in[bass.ds(qv, 1)].rearrange("a s n -> (a s) n")
        msrc = mk.mem_in[bass.ds(qv, 1)].rearrange("a s n -> (a s) n")
        ydst = mk.y_out[bass.ds(qv, 1)].rearrange("a s n -> (a s) n")
        for tt in range(NT):
            sc.dma("sp", mk.xb[0][tt * TT:(tt + 1) * TT, :].rearrange("(s p) n -> p s n", p=128),
                   xsrc[tt * TT:(tt + 1) * TT, :].rearrange("(s p) n -> p s n", p=128), b_cp, writes=[mk.b_xb[0][tt]])
        sc.dma("sp", mk.memc.rearrange("(s p) n -> p s n", p=128), msrc.rearrange("(s p) n -> p s n", p=128), b_cp,
               writes=[mk.b_memc])
        try:
            for l in range(nlayer):
                mk.layer_consts(l)
                mk.mem_kv(l, 0)
                mk.p1(l, 0, mk.xb, mk.b_xb)
                mk.p23(l, 0, mk.xb, mk.b_xb)
                mk.p4(l, 0)
            mk.final(0, ydst)
        except StopBuild:
            pass
        mk.barrier()
        sc.run()
        sc.clear_sems()
    st.close()
    return nc, mk


def host_consts(inp):
    L = NL

    def colmajor(a):
        a = np.asarray(a, np.float32)
        c = a.shape[1] // 128
        return np.ascontiguousarray(a.reshape(L, c, 128).transpose(2, 0, 1).reshape(128, L * c))

    colp = np.zeros((128, NCOL), np.float32)
    for name, off in (("norm_mix_g", O_GMIX), ("norm_x_g", O_GX), ("norm_mem_g", O_GMEM), ("norm_ffn_g", O_GFFN),
                      ("sg_norm_g", O_SGG), ("sg_norm_b", O_SGB)):
        colp[:, off:off + L * NCH] = colmajor(inp[name])
    colp[:, O_QG:O_QG + L] = np.asarray(inp["q_norm_g"], np.float32).T
    colp[:, O_KG:O_KG + L] = np.asarray(inp["k_norm_g"], np.float32).T
    cw = np.asarray(inp["conv_w"], np.float32).reshape(L, 3, NF, 128).transpose(3, 0, 1, 2).reshape(128, L * 3 * NF)
    colp[:, O_CW:O_CW + L * 3 * NF] = cw
    colp[:, O_CB:O_CB + L * NF] = colmajor(inp["conv_b"])
    wst = np.ascontiguousarray(np.asarray(inp["w_spatial"], np.float32).transpose(0, 3, 1, 2).reshape(L, 128, 1024))
    bsp = np.ascontiguousarray(np.asarray(inp["b_spatial"], np.float32).reshape(L, 1, 1024))
    fng = np.ascontiguousarray(np.asarray(inp["final_norm_g"], np.float32).reshape(1, D))
    t = np.arange(S)
    row = (t // 64).astype(np.float32)
    colv = (t % 64).astype(np.float32)
    quarter = HD // 4
    inv = (np.float32(10000.0) ** (-np.arange(quarter, dtype=np.float32) / np.float32(quarter))).astype(np.float32)
    ang_r = row[:, None] * inv[None, :]
    ang_c = colv[:, None] * inv[None, :]
    ang = np.concatenate([ang_r, ang_r, ang_c, ang_c], axis=-1).astype(np.float32)
    cosT = np.ascontiguousarray(np.cos(ang).T.astype(np.float32))
    sinT = np.sin(ang).T.astype(np.float32)
    sign = np.ones((128, 1), np.float32)
    sign[32:64] = -1.0
    sign[96:128] = -1.0
    sinP = np.ascontiguousarray(sinT * sign)
    return dict(colp=colp, wst=wst, bsp=bsp, fng=fng, cosT=cosT, sinP=sinP, ident=np.eye(128, dtype=np.float32))


W_NAMES = ("w_in", "w_attn_o", "w_sg_o", "w_out", "w_xq", "w_xkv", "w_xo", "w_ffn_up", "w_ffn_down")
_CACHE = {}


def kernel(**inp):
    xs = [np.asarray(inp["x_prompt"][i], np.float32) for i in range(4)] + [np.asarray(inp["x_sample"][i], np.float32) for i in range(16)]
    ms = [np.asarray(inp["mem_prompt"][i], np.float32) for i in range(4)] + [np.asarray(inp["mem_sample"][i], np.float32) for i in range(16)]
    nseq = SEQ_PER_CORE
    if "nc" not in _CACHE:
        _CACHE["nc"] = build(nseq, NL)[0]
    nc = _CACHE["nc"]
    consts = host_consts(inp)
    weights = {n: np.ascontiguousarray(np.asarray(inp[n], np.float32)) for n in W_NAMES}
    in_maps = []
    slots = []
    for c in range(NCORES):
        ids = [c + NCORES * j for j in range(nseq)]
        slots.append(ids)
        x = np.zeros((nseq, S, D), np.float32)
        m = np.zeros((nseq, NMEM, D), np.float32)
        for j, i in enumerate(ids):
            if i < len(xs):
                x[j] = xs[i]
                m[j] = ms[i]
        d = {"x": x, "mem": m}
        d.update(weights)
        d.update(consts)
        in_maps.append(d)
    res = run_bass_kernel_spmd(nc, in_maps, core_ids=list(range(NCORES)))
    ys = [None] * 20
    for c in range(NCORES):
        y = res.results[c]["y"]
        for j, i in enumerate(slots[c]):
            if i < 20:
                ys[i] = np.asarray(y[j], np.float32)
    y_prompt = np.stack(ys[0:4], axis=0)
    y_sample = np.stack(ys[4:20], axis=0)
    return (y_prompt, y_sample)
```

```python
import numpy as np
from contextlib import ExitStack
import concourse.bass as bass
import concourse.mybir as mybir
from concourse.bass_utils import run_bass_kernel_spmd

F32 = mybir.dt.float32
BF16 = mybir.dt.bfloat16
AF = mybir.ActivationFunctionType
ALU = mybir.AluOpType

D = 2048
S = 2048
NL = 4
HD = 128
NH = 16
NKV = 4
DFF = 5632
NMEM = 256
NIN = 11264
TT = 512
NT = S // TT
NCH = D // 128
NF = DFF // 128
EPS = 1e-6
NCORES = 8
SEQ_PER_CORE = 3

EPOCH = 16000
SAME_SYNC = True
ENGS = ("pe", "act", "dve", "pool", "sp")


class Buf:
    __slots__ = ("name", "w", "r", "dsem", "dcnt", "al", "pr", "excl")

    def __init__(self, name, dsem=None):
        self.name = name
        self.w = []
        self.r = {}
        self.dsem = dsem
        self.dcnt = 0
        self.al = []
        self.pr = {}
        self.excl = False


class DSem:
    __slots__ = ("sem",)

    def __init__(self, sem):
        self.sem = sem


DMA_SEM_MAX = 1800


class Sched:
    def __init__(self, nc, stack):
        self.nc = nc
        self.stack = stack
        self.lists = {e: [] for e in ENGS}
        self.count = {e: 0 for e in ENGS}
        self.sems = {e: [] for e in ENGS}
        self.known = {e: {} for e in ENGS}
        self.nsem = 0
        self.all_sems = []
        self.all_bufs = []

    def new_sem(self, name):
        self.nsem += 1
        sem = self.stack.enter_context(self.nc.semaphore(name))
        self.all_sems.append(sem)
        return sem

    def buf(self, name, dma=False):
        b = Buf(name, DSem(self.new_sem("d_" + name)) if dma else None)
        self.all_bufs.append(b)
        return b

    def reset(self):
        self.lists = {e: [] for e in ENGS}
        self.count = {e: 0 for e in ENGS}
        self.known = {e: {} for e in ENGS}
        for b in self.all_bufs:
            b.w = []
            b.r = {}
            b.pr = {}
            b.dcnt = 0

    def clear_sems(self):
        for sm in self.all_sems:
            self.nc.sync.sem_clear(sm)
        self.nc.all_engine_barrier()

    def _sem_for(self, eng, idx):
        ep = idx // EPOCH
        while len(self.sems[eng]) <= ep:
            self.sems[eng].append(self.new_sem("c_%s_%d" % (eng, len(self.sems[eng]))))
        return self.sems[eng][ep], idx % EPOCH + 1

    def _waits(self, eng, reads, writes, extra, pw=()):
        raw = []
        oth = []
        for b in reads:
            raw.extend(b.w)
            if b.excl:
                oth.extend(ev for k, ev in b.r.items() if k != eng)
            for a in b.al:
                raw.extend(a.w)
        for b in writes:
            oth.extend(b.w)
            oth.extend(b.r.values())
            for a in b.al:
                oth.extend(a.w)
                oth.extend(a.r.values())
        for b in pw:
            oth.extend(b.r.values())
            if not b.r:
                oth.extend(b.pr.values())
            for a in b.al:
                oth.extend(a.w)
                oth.extend(a.r.values())
        raw.extend(extra)
        cmax = {}
        dmax = {}
        for is_raw, deps in ((True, raw), (False, oth)):
            for ev in deps:
                if ev[0] == "c":
                    _, e2, idx = ev
                    if e2 == eng and (eng in ("pe", "sp") or not SAME_SYNC or not is_raw):
                        continue
                    if idx > cmax.get(e2, -1):
                        cmax[e2] = idx
                else:
                    _, ds, cnt = ev
                    if cnt > dmax.get(ds, 0):
                        dmax[ds] = cnt
        out = []
        kn = self.known[eng]
        for e2, idx in cmax.items():
            if kn.get(e2, -1) >= idx:
                continue
            kn[e2] = idx
            out.append(self._sem_for(e2, idx))
        for ds, cnt in dmax.items():
            if kn.get(ds, 0) >= cnt:
                continue
            kn[ds] = cnt
            out.append((ds.sem, cnt * 16))
        return out

    def _record(self, key, ev, reads, writes, pw):
        for b in reads:
            b.r[key] = ev
        for b in writes:
            b.w = [ev]
            b.r = {}
            b.pr = {}
        for b in pw:
            if b.r:
                b.w = [ev]
                b.pr = b.r
                b.r = {}
            else:
                b.w.append(ev)

    def op(self, eng, body, reads=(), writes=(), extra=(), pw=()):
        waits = self._waits(eng, reads, writes, extra, pw)
        idx = self.count[eng]
        self.count[eng] = idx + 1
        sem, _ = self._sem_for(eng, idx)
        ev = ("c", eng, idx)

        def emit(e, waits=waits, body=body, sem=sem):
            for sm, v in waits:
                e.wait_ge(sm, v)
            body(e).then_inc(sem, 1)

        self.lists[eng].append(emit)
        self._record(eng, ev, reads, writes, pw)
        return ev

    def dma(self, q, out_ap, in_ap, owner, reads=(), writes=(), extra=()):
        waits = self._waits(q, reads, writes, extra)
        if owner.dcnt >= DMA_SEM_MAX:
            owner.dsem = DSem(self.new_sem("d_" + owner.name))
            owner.dcnt = 0
        owner.dcnt += 1
        ev = ("d", owner.dsem, owner.dcnt)
        sem = owner.dsem.sem

        def emit(e, waits=waits, sem=sem, out_ap=out_ap, in_ap=in_ap):
            for sm, v in waits:
                e.wait_ge(sm, v)
            e.dma_start(out=out_ap, in_=in_ap).then_inc(sem, 16)

        self.lists[q].append(emit)
        self._record(("dq", owner), ev, reads, writes, ())
        return ev

    def final_wait(self, eng, events):
        waits = self._waits(eng, (), (), events)

        def emit(e, waits=waits):
            for sm, v in waits:
                e.wait_ge(sm, v)

        self.lists[eng].append(emit)

    def run(self):
        nc = self.nc
        with nc.Block() as block:
            @block.tensor
            def _(e):
                for f in self.lists["pe"]:
                    f(e)

            @block.scalar
            def _(e):
                for f in self.lists["act"]:
                    f(e)

            @block.vector
            def _(e):
                for f in self.lists["dve"]:
                    f(e)

            @block.gpsimd
            def _(e):
                for f in self.lists["pool"]:
                    f(e)

            @block.sync
            def _(e):
                for f in self.lists["sp"]:
                    f(e)


O_GMIX = 0
O_GX = O_GMIX + NL * NCH
O_GMEM = O_GX + NL * NCH
O_GFFN = O_GMEM + NL * NCH
O_SGG = O_GFFN + NL * NCH
O_SGB = O_SGG + NL * NCH
O_QG = O_SGB + NL * NCH
O_KG = O_QG + NL
O_CW = O_KG + NL
O_CB = O_CW + NL * 3 * NF
NCOL = O_CB + NL * NF

B_Q, B_K, B_V, B_U, B_SV, B_GA, B_GS = 0, 16, 20, 24, 40, 56, 72
ARENA_BYTES = 206 * 1024
RING = 10


class StopBuild(Exception):
    pass


class MK:
    def __init__(self, nc, st, nseq, nlayer):
        self.nc, self.st, self.NSEQ, self.NLY = nc, st, nseq, nlayer
        self.sc = Sched(nc, st)
        sc = self.sc
        dt = nc.dram_tensor
        I = lambda name, shape: dt(name, shape, F32, kind="ExternalInput").ap()
        self.x_in = I("x", [nseq, S, D])
        self.mem_in = I("mem", [nseq, NMEM, D])
        self.w = {}
        for name, k, n in (("w_in", D, NIN), ("w_attn_o", D, D), ("w_sg_o", D, D), ("w_out", D, D),
                           ("w_xq", D, D), ("w_xkv", D, 2 * D), ("w_xo", D, D), ("w_ffn_up", D, 2 * DFF),
                           ("w_ffn_down", DFF, D)):
            self.w[name] = I(name, [NL, k, n])
        self.colp_in = I("colp", [128, NCOL])
        self.wst_in = I("wst", [NL, 128, 1024])
        self.bsp_in = I("bsp", [NL, 1, 1024])
        self.fng_in = I("fng", [1, D])
        self.cos_in = I("cosT", [128, S])
        self.sin_in = I("sinP", [128, S])
        self.ident_in = I("ident", [128, 128])
        self.y_out = dt("y", [nseq, S, D], F32, kind="ExternalOutput").ap()
        Sc = lambda name, shape, d=BF16: dt(name, shape, d).ap()
        self.ws = {n: Sc("s_" + n, [nlayer, nb, 128, 2048]) for n, nb in
                   (("w_in", 88), ("w_attn_o", 16), ("w_sg_o", 16), ("w_xq", 16), ("w_xkv", 32), ("w_ffn_up", 88))}
        self.wm = {n: Sc("m_" + n, [nlayer, 4, ng, 128, 2048]) for n, ng in
                   (("w_sv", 4), ("w_out", 4), ("w_xo", 4), ("w_ffn_down", 11))}
        self.xa = Sc("xa", [1, S, D], F32)
        self.xb = Sc("xb", [1, S, D], F32)
        self.memc = Sc("memc", [NMEM, D], F32)
        self.b_memc = sc.buf("memc")
        self.kts = Sc("kts", [NKV, 128, S])
        self.vs = Sc("vs", [NKV, 128, NCH, 128])
        self.kms = Sc("kms", [4, 128, 1024])
        self.vms = Sc("vms", [4, 128, 1024])
        self.b_xa = [[sc.buf("xa%d_%d" % (q, t)) for t in range(NT)] for q in range(nseq)]
        self.b_xb = [[sc.buf("xb%d_%d" % (q, t)) for t in range(NT)] for q in range(nseq)]
        self.b_kts = [sc.buf("kts%d" % g) for g in range(NKV)]
        self.b_vs = [sc.buf("vs%d" % g) for g in range(NKV)]
        self.b_kms = [sc.buf("kms%d" % g) for g in range(4)]
        self.b_vms = [sc.buf("vms%d" % g) for g in range(4)]
        self.arena = st.enter_context(nc.sbuf_tensor("arena", [128, ARENA_BYTES // 2], BF16))
        self.aoff = 0
        self.ps = [st.enter_context(nc.psum_tensor("ps%d" % i, [128, 512], F32)) for i in range(8)]
        self.psbf = [p.bitcast(BF16) for p in self.ps]
        self.b_ps = [sc.buf("ps%d" % i) for i in range(8)]
        for b in self.b_ps:
            b.excl = True
        self.dma_bufs = []
        self.prologue_events = []
        self.debug = False
        self.dbg_out = {}

    def alloc(self, nbytes):
        off = self.aoff
        self.aoff += (nbytes + 63) // 64 * 64
        assert self.aoff <= ARENA_BYTES, self.aoff
        return off

    def vf(self, off, n):
        return self.arena[:, off // 2: off // 2 + 2 * n].bitcast(F32)

    def vb(self, off, n):
        return self.arena[:, off // 2: off // 2 + n]

    def dbuf(self, name):
        b = self.sc.buf(name, dma=True)
        self.dma_bufs.append(b)
        return b

    def dbg(self, name, ap, rbufs, dtype=None):
        if not getattr(self, "debug", False) or name in self.dbg_out:
            return
        n = ap.shape[1] if len(ap.shape) == 2 else int(np.prod(ap.shape[1:]))
        dtp = dtype or ap.tensor.dtype
        t = self.nc.dram_tensor("dbg_" + name, [128] + list(ap.shape[1:]), dtp, kind="ExternalOutput").ap()
        b = self.dbuf("dbg_" + name)
        self.dbg_out[name] = self.sc.dma("sp", t, ap, b, reads=rbufs)
        if name == getattr(self, "stop", None):
            raise StopBuild()

    def barrier(self):
        sc = self.sc
        evs = [("c", e, sc.count[e] - 1) for e in ENGS if sc.count[e] > 0]
        evs += [("d", b.dsem, b.dcnt) for b in self.dma_bufs if b.dcnt > 0]
        for e in ENGS:
            sc.final_wait(e, evs)

    def prologue(self):
        sc = self.sc
        save = self.aoff
        NS = 3
        st32 = [self.alloc(32768) for _ in range(NS)]
        st16 = [self.alloc(16384) for _ in range(NS)]
        b32 = [self.dbuf("st32_%d" % i) for i in range(NS)]
        b16 = [self.dbuf("st16_%d" % i) for i in range(NS)]
        cast_engs = ("dve", "act", "dve", "act")
        units = []

        def unit(src_ap, nch, dst_ap, blocked):
            units.append((src_ap, nch, dst_ap, blocked))

        def emit_load(i):
            src_ap, nch, dst_ap, blocked = units[i]
            k = i % NS
            n = nch * 512
            v32 = self.vf(st32[k], n)
            v16 = self.vb(st16[k], n)
            sc.dma("sp", v32.rearrange("p (c n) -> p c n", c=nch), src_ap, b32[k], writes=[b32[k]])
            if blocked:
                o = v16.rearrange("p (m c n) -> p c m n", m=4, c=nch, n=128)
                iv = v32.rearrange("p (c m n) -> p c m n", c=nch, m=4, n=128)
            else:
                o, iv = v16, v32
            eng = cast_engs[i % 4]
            if eng == "act":
                sc.op("act", lambda e, o=o, iv=iv: e.activation(out=o, in_=iv, func=AF.Copy),
                      reads=[b32[k]], writes=[b16[k]])
            else:
                sc.op(eng, lambda e, o=o, iv=iv: e.tensor_copy(out=o, in_=iv),
                      reads=[b32[k]], writes=[b16[k]])

        def emit_store(i):
            src_ap, nch, dst_ap, blocked = units[i]
            k = i % NS
            v16 = self.vb(st16[k], nch * 512)
            sc.dma("sp", dst_ap, v16.rearrange("p (g f) -> p g f", f=2048), b16[k], reads=[b16[k]])

        for l in range(self.NLY):
            for name in ("w_in", "w_attn_o", "w_sg_o", "w_xq", "w_xkv", "w_ffn_up"):
                W = self.w[name][l]
                ncols = W.shape[1]
                for j in range(ncols // 512):
                    src = W[:, j * 512:(j + 1) * 512].rearrange("(c p) n -> p c n", p=128)
                    dst = self.ws[name][l, j * 4:(j + 1) * 4].rearrange("m p f -> p m f")
                    unit(src, 16, dst, True)
            for name, srcname, coff in (("w_sv", "w_in", B_SV * 128), ("w_out", "w_out", 0), ("w_xo", "w_xo", 0)):
                W = self.w[srcname][l]
                for j in range(4):
                    src = W[:, coff + j * 512: coff + (j + 1) * 512].rearrange("(c p) n -> p c n", p=128)
                    dst = self.wm[name][l, j, 0:4].rearrange("g p f -> p g f")
                    unit(src, 16, dst, False)
            W = self.w["w_ffn_down"][l]
            for j in range(4):
                for g0, ng in ((0, 4), (4, 4), (8, 3)):
                    src = W[g0 * 512:(g0 + ng) * 512, j * 512:(j + 1) * 512].rearrange("(c p) n -> p c n", p=128)
                    dst = self.wm["w_ffn_down"][l, j, g0:g0 + ng].rearrange("g p f -> p g f")
                    unit(src, ng * 4, dst, False)
        LAG = 1
        for i in range(len(units) + LAG):
            if i < len(units):
                emit_load(i)
            if i - LAG >= 0:
                emit_store(i - LAG)
        self.barrier()
        self.aoff = save

    def setup_main(self):
        sc = self.sc
        A = self.alloc
        self.COLP = self.vf(A(NCOL * 4), NCOL); self.b_colp = self.dbuf("colp")
        self.IDENT = self.vb(A(256), 128); self.b_ident = sc.buf("ident")
        self.ONES = self.vb(A(256), 128); self.b_ones = sc.buf("ones")
        self.COS = self.vf(A(2048), 512); self.b_cos = self.dbuf("cos")
        self.SIN = self.vf(A(2048), 512); self.b_sin = self.dbuf("sin")
        self.WST = self.vb(A(2048), 1024); self.b_wst = sc.buf("wst")
        self.BSB = self.vf(A(4096), 1024); self.b_bsb = self.dbuf("bsb")
        self.RSS = self.vf(A(4096), 1024); self.b_rss = sc.buf("rss")
        self.XT = self.vf(A(32768), 8192); self.b_xt = self.dbuf("xt")
        self.XT3 = self.XT.rearrange("p (s n) -> p s n", s=4)
        self.HALO = self.vf(A(8192), 2048); self.b_halo = self.dbuf("halo")
        self.HBs = [self.vb(A(4096), 2048) for _ in range(2)]
        self.b_hbs = [sc.buf("hb%d" % i) for i in range(2)]
        self.HB, self.b_hb = self.HBs[0], self.b_hbs[0]
        self.norm_i = 0
        self.HT = self.vb(A(16384), 8192); self.b_ht = sc.buf("ht")
        self.HT3 = self.HT.rearrange("p (c n) -> p c n", c=NCH)
        self.HTH = self.vb(A(64), 32); self.b_hth = sc.buf("hth")
        self.HTH3 = self.HTH.rearrange("p (c n) -> p c n", c=NCH)
        r1 = A(49152)
        self.B1 = self.vb(r1, 8192); self.b_b1 = self.dbuf("b1")
        self.B2 = self.vb(r1 + 16384, 8192); self.b_b2 = self.dbuf("b2")
        self.B3 = self.vb(r1 + 32768, 8192); self.b_b3 = self.dbuf("b3")
        self.B3_off = r1 + 32768
        self.B1_3 = self.B1.rearrange("p (c n) -> p c n", c=NCH)
        self.B2_3 = self.B2.rearrange("p (c n) -> p c n", c=NCH)
        self.GV3 = self.B3.rearrange("p (s n) -> p s n", s=4)
        self.G = self.vb(r1, NF * 512); self.b_g = sc.buf("g")
        self.G3 = self.G.rearrange("p (f n) -> p f n", f=NF)
        self.GBC = self.vf(r1 + 32768, 2048)
        self.b_gbc = self.dbuf("gbc")
        self.b_g.al = [self.b_b1, self.b_b2, self.b_b3, self.b_gbc]
        self.b_b1.al = [self.b_g]
        self.b_b2.al = [self.b_g]
        self.b_b3.al = [self.b_g, self.b_gbc]
        self.b_gbc.al = [self.b_b3, self.b_g]
        self.ring = [self.vb(A(4096), 2048) for _ in range(RING)]
        self.b_ring = [self.dbuf("ring%d" % i) for i in range(RING)]
        self.ring_i = 0
        self.f32r = [self.vf(A(2048), 512) for _ in range(8)]
        self.b_f32r = [sc.buf("f32r%d" % i) for i in range(8)]
        self.f32_i = 0
        self.bfr = [self.vb(A(1024), 512) for _ in range(10)]
        self.b_bfr = [self.dbuf("bfr%d" % i) for i in range(10)]
        self.bf_i = 0
        self.QT = [self.vb(A(1024), 512) for _ in range(2)]
        self.b_qt = [sc.buf("qt%d" % i) for i in range(2)]
        self.ABUF = [self.vf(A(2080), 514) for _ in range(2)]
        self.b_abuf = [sc.buf("abuf%d" % i) for i in range(2)]
        self.SM = self.vf(A(128), 32)
        self.b_sm = [sc.buf("sm%d" % i) for i in range(32)]
        sc.dma("sp", self.COLP, self.colp_in, self.b_colp, writes=[self.b_colp])
        tmp = self.f32r[0]
        sc.dma("sp", tmp[:, 0:128], self.ident_in, self.b_cos, writes=[self.b_f32r[0]])
        sc.op("dve", lambda e: e.tensor_copy(out=self.IDENT, in_=tmp[:, 0:128]), reads=[self.b_f32r[0]], writes=[self.b_ident])
        sc.op("pool", lambda e: e.memset(self.ONES, 1.0), writes=[self.b_ones])

    def col(self, off):
        return self.COLP[:, off:off + 1]

    def slot(self):
        i = self.ring_i
        self.ring_i = (i + 1) % RING
        return self.ring[i], self.b_ring[i]

    def f32t(self):
        i = self.f32_i
        self.f32_i = (i + 1) % 8
        return self.f32r[i], self.b_f32r[i]

    def bft(self):
        i = self.bf_i
        self.bf_i = (i + 1) % 10
        return self.bfr[i], self.b_bfr[i]

    def load_slot(self, src_ap, reads=()):
        t, b = self.slot()
        self.sc.dma("sp", t, src_ap, b, reads=reads, writes=[b])
        return t, b

    def lin_ws(self, wname, l, blk, bank, rhs3, rhs_b, ncols=512, extra_pe=None):
        sc = self.sc
        t, b = self.load_slot(self.ws[wname][l, blk])
        t3 = t.rearrange("p (c n) -> p c n", c=NCH)
        ps = self.ps[bank]

        def body(e, t3=t3, ps=ps, rhs3=rhs3, ncols=ncols):
            r = None
            for c in range(NCH):
                r = e.matmul(ps[:, 0:ncols], lhsT=t3[:, c, :], rhs=rhs3[:, c, 0:ncols], start=(c == 0), stop=(c == NCH - 1))
            return r

        sc.op("pe", body, reads=[b, rhs_b], writes=[self.b_ps[bank]])
        return t3, b

    def lin_moving_residual(self, wname, l, in3, in_b, ngroups):
        sc = self.sc
        for j in range(4):
            base = (j % 2) * 4
            for g4 in range(ngroups):
                t, b = self.load_slot(self.wm[wname][l, j, g4])
                t3 = t.rearrange("p (c n) -> p c n", c=4)
                for s_ in range(4):
                    def body(e, t3=t3, s_=s_, g4=g4, base=base):
                        r = None
                        for c in range(4):
                            r = e.matmul(self.ps[base + s_][:, :], lhsT=in3[:, g4 * 4 + c, s_ * 128:(s_ + 1) * 128],
                                         rhs=t3[:, c, :], start=(g4 == 0 and c == 0),
                                         stop=(g4 == ngroups - 1 and c == 3))
                        return r
                    sc.op("pe", body, reads=[b, in_b], writes=[self.b_ps[base + s_]])
            for s_ in range(4):
                xs = self.XT3[:, s_, j * 512:(j + 1) * 512]
                sc.op("dve", lambda e, xs=xs, p=self.ps[base + s_]: e.tensor_tensor(out=xs, in0=p[:, :], in1=xs, op=ALU.add),
                      reads=[self.b_ps[base + s_], self.b_xt], writes=[self.b_xt])

    def norm_sub(self, xap, xbuf, goff, dest, dest_b, ncols, banks=(0, 1)):
        sc = self.sc
        i = self.norm_i
        self.norm_i += 1
        HB, b_hb = self.HBs[i % 2], self.b_hbs[i % 2]
        k0 = 16 + 3 * (i % 4)
        SS, RT, RS = self.SM[:, k0:k0 + 1], self.SM[:, k0 + 1:k0 + 2], self.SM[:, k0 + 2:k0 + 3]
        bss, brt, brs = self.b_sm[k0], self.b_sm[k0 + 1], self.b_sm[k0 + 2]
        sc.op("act", lambda e: e.activation(out=HB, in_=xap, func=AF.Square, accum_out=SS),
              reads=[xbuf], writes=[b_hb, bss])
        sc.op("act", lambda e: e.activation(out=RT, in_=SS, func=AF.Ln, scale=1.0 / D, bias=EPS), reads=[bss], writes=[brt])
        sc.op("act", lambda e: e.activation(out=RS, in_=RT, func=AF.Exp, scale=-0.5), reads=[brt], writes=[brs])
        sc.op("dve", lambda e: e.tensor_scalar(out=HB, in0=xap, scalar1=RS, scalar2=None, op0=ALU.mult),
              reads=[xbuf, brs], writes=[b_hb])
        for half in range(2):
            bank = banks[half]
            pb = self.psbf[bank]

            def body(e, half=half, pb=pb):
                r = None
                for k in range(8):
                    c = half * 8 + k
                    r = e.transpose(out=pb[:, k * 128:(k + 1) * 128], in_=HB[:, c * 128:(c + 1) * 128], identity=self.IDENT)
                return r
            sc.op("pe", body, reads=[b_hb, self.b_ident], writes=[self.b_ps[bank]])
            for k in range(8):
                c = half * 8 + k
                src = pb[:, k * 128:k * 128 + ncols]
                d = dest(c)
                g = self.col(goff + c)
                if half == 0:
                    sc.op("act", lambda e, d=d, src=src, g=g: e.activation(out=d, in_=src, func=AF.Copy, scale=g),
                          reads=[self.b_ps[bank], self.b_colp], pw=[dest_b])
                else:
                    sc.op("dve", lambda e, d=d, src=src, g=g: e.tensor_scalar(out=d, in0=src, scalar1=g, scalar2=None, op0=ALU.mult),
                          reads=[self.b_ps[bank], self.b_colp], pw=[dest_b])

    def norm_tile(self, goff):
        for s_ in range(4):
            self.norm_sub(self.XT3[:, s_, :], self.b_xt, goff,
                          lambda c, s_=s_: self.HT3[:, c, s_ * 128:(s_ + 1) * 128], self.b_ht, 128,
                          banks=((0, 1) if s_ % 2 == 0 else (2, 3)))

    def qk_rope(self, bank, sumbank, gcol, dest, dest_b):
        sc = self.sc
        ps, psb = self.ps[bank], self.b_ps[bank]
        sq, bsq = self.bft()
        sc.op("act", lambda e: e.activation(out=sq, in_=ps[:, :], func=AF.Square), reads=[psb], writes=[bsq])
        p2, p2b = self.ps[sumbank], self.b_ps[sumbank]
        sc.op("pe", lambda e: e.matmul(p2[:, :], lhsT=self.ONES, rhs=sq, start=True, stop=True),
              reads=[bsq, self.b_ones], writes=[p2b])
        rt, brt = self.f32t()
        sc.op("act", lambda e: e.activation(out=rt, in_=p2[:, :], func=AF.Ln, scale=1.0 / HD, bias=EPS), reads=[p2b], writes=[brt])
        sc.op("act", lambda e: e.activation(out=rt, in_=rt, func=AF.Exp, scale=-0.5), reads=[brt], writes=[brt])
        kn, bkn = self.f32t()
        sc.op("dve", lambda e: e.scalar_tensor_tensor(out=kn, in0=ps[:, :], scalar=gcol, in1=rt, op0=ALU.mult, op1=ALU.mult),
              reads=[psb, brt, self.b_colp], writes=[bkn])
        t1, bt1 = self.f32t()
        sc.op("pool", lambda e: e.tensor_tensor(out=t1, in0=kn, in1=self.COS, op=ALU.mult), reads=[bkn, self.b_cos], writes=[bt1])
        t2, bt2 = self.f32t()
        for qd in range(4):
            pq = qd ^ 1
            sc.op("pool", lambda e, qd=qd, pq=pq: e.tensor_tensor(out=t2[qd * 32:(qd + 1) * 32, :], in0=kn[pq * 32:(pq + 1) * 32, :],
                                                                 in1=self.SIN[pq * 32:(pq + 1) * 32, :], op=ALU.mult),
                  reads=[bkn, self.b_sin], writes=[bt2])
        sc.op("pool", lambda e: e.tensor_tensor(out=dest, in0=t1, in1=t2, op=ALU.add), reads=[bt1, bt2], writes=[dest_b])

    def load_x_tile(self, src_ap, src_bufs, tt):
        self.sc.dma("sp", self.XT3, src_ap[tt * TT:(tt + 1) * TT, :].rearrange("(s p) n -> p s n", p=128),
                    self.b_xt, reads=src_bufs, writes=[self.b_xt])

    def store_x_tile(self, dst_ap, dst_bufs, tt):
        return self.sc.dma("sp", dst_ap[tt * TT:(tt + 1) * TT, :].rearrange("(s p) n -> p s n", p=128), self.XT3,
                           self.b_xt, reads=[self.b_xt], writes=dst_bufs)

    def load_cs(self, tt):
        sc = self.sc
        sc.dma("sp", self.COS, self.cos_in[:, tt * TT:(tt + 1) * TT], self.b_cos, writes=[self.b_cos])
        sc.dma("sp", self.SIN, self.sin_in[:, tt * TT:(tt + 1) * TT], self.b_sin, writes=[self.b_sin])

    def layer_consts(self, l):
        sc = self.sc
        stg = self.vf(self.B3_off, 1024)
        sc.dma("sp", stg, self.wst_in[l], self.b_b3, writes=[self.b_b3])
        sc.op("dve", lambda e: e.tensor_copy(out=self.WST, in_=stg), reads=[self.b_b3], writes=[self.b_wst])
        sc.dma("sp", self.BSB, self.bsp_in[l].partition_broadcast(128), self.b_bsb, writes=[self.b_bsb])
        for h in range(2):
            sc.op("pe", lambda e, h=h: e.matmul(self.ps[h][:, :], lhsT=self.ONES, rhs=self.WST[:, h * 512:(h + 1) * 512],
                                                start=True, stop=True),
                  reads=[self.b_ones, self.b_wst], writes=[self.b_ps[h]])
            sc.op("act", lambda e, h=h: e.activation(out=self.RSS[:, h * 512:(h + 1) * 512], in_=self.ps[h][:, :], func=AF.Copy),
                  reads=[self.b_ps[h]], writes=[self.b_rss])

    def mem_kv(self, l, q):
        sc = self.sc
        sc.dma("sp", self.XT3[:, 0:2, :], self.memc.rearrange("(s p) n -> p s n", p=128), self.b_xt, reads=[self.b_memc], writes=[self.b_xt])
        for s_ in range(2):
            self.norm_sub(self.XT3[:, s_, :], self.b_xt, O_GMEM + l * NCH,
                          lambda c, s_=s_: self.HT3[:, c, s_ * 128:(s_ + 1) * 128], self.b_ht, 128)
        KST = self.B1.rearrange("p (m n) -> p m n", m=32)
        VST = self.B2.rearrange("p (k n) -> p k n", k=4)
        for m in range(NCH):
            bank = m % 2
            self.lin_ws("w_xkv", l, m, bank, self.HT3, self.b_ht, ncols=256)
            sc.op("act", lambda e, m=m, bank=bank: e.activation(out=KST[:, m, :], in_=self.ps[bank][:, 0:256], func=AF.Copy),
                  reads=[self.b_ps[bank]], writes=[self.b_b1])
        for hx in range(4):
            sc.dma("sp", self.kms[hx].rearrange("p (m n) -> p m n", m=4), KST[:, hx * 4:(hx + 1) * 4, :], self.b_b1,
                   reads=[self.b_b1], writes=[self.b_kms[hx]])
        for m in range(NCH):
            bank = 2 + m % 2
            self.lin_ws("w_xkv", l, NCH + m, bank, self.HT3, self.b_ht, ncols=256)
            vt, bvt = self.bft()
            sc.op("act", lambda e, bank=bank, vt=vt: e.activation(out=vt[:, 0:256], in_=self.ps[bank][:, 0:256], func=AF.Copy),
                  reads=[self.b_ps[bank]], writes=[bvt])
            tb = 4 + m % 2
            pb = self.psbf[tb]

            def body(e, vt=vt, pb=pb):
                e.transpose(out=pb[:, 0:128], in_=vt[:, 0:128], identity=self.IDENT)
                return e.transpose(out=pb[:, 128:256], in_=vt[:, 128:256], identity=self.IDENT)
            sc.op("pe", body, reads=[bvt, self.b_ident], writes=[self.b_ps[tb]])
            sc.op("dve", lambda e, m=m, pb=pb: e.tensor_copy(out=VST[:, 0:2, m * 128:(m + 1) * 128],
                                                             in_=pb[:, 0:256].rearrange("p (k n) -> p k n", k=2)),
                  reads=[self.b_ps[tb]], writes=[self.b_b2])
        for hx in range(4):
            sc.dma("sp", self.vms[hx].rearrange("p (k n) -> p k n", k=2), VST[:, 0:2, hx * 512:(hx + 1) * 512], self.b_b2,
                   reads=[self.b_b2], writes=[self.b_vms[hx]])

    def p1(self, l, q, xsrc, xsrc_b):
        sc = self.sc
        for tt in range(NT):
            self.load_x_tile(xsrc[q], [xsrc_b[q][tt]], tt)
            self.load_cs(tt)
            self.norm_tile(O_GMIX + l * NCH)
            self.dbg("p1_ht", self.HT, [self.b_ht])
            pending = []
            for g in range(NKV):
                kbank = g % 2
                vbank = 4 + g % 2
                self.lin_ws("w_in", l, B_K + g, kbank, self.HT3, self.b_ht)
                self.lin_ws("w_in", l, B_V + g, vbank, self.HT3, self.b_ht)
                for fn in pending:
                    fn()
                pending = []
                kt, bkt = self.bft()
                self.qk_rope(kbank, 2 + g % 2, self.col(O_KG + l), kt, bkt)
                self.dbg("p1_kt", kt, [bkt])
                pending.append(lambda g=g, kt=kt, bkt=bkt, tt=tt: sc.dma("sp", self.kts[g][:, tt * TT:(tt + 1) * TT], kt, bkt,
                                                                         reads=[bkt], writes=[self.b_kts[g]]))
                vt, bvt = self.bft()
                sc.op("act", lambda e, vbank=vbank, vt=vt: e.activation(out=vt, in_=self.ps[vbank][:, :], func=AF.Copy),
                      reads=[self.b_ps[vbank]], writes=[bvt])
                tb = 6 + g % 2
                pb = self.psbf[tb]

                def body(e, vt=vt, pb=pb):
                    r = None
                    for k in range(4):
                        r = e.transpose(out=pb[:, k * 128:(k + 1) * 128], in_=vt[:, k * 128:(k + 1) * 128], identity=self.IDENT)
                    return r
                sc.op("pe", body, reads=[bvt, self.b_ident], writes=[self.b_ps[tb]])
                vo, bvo = self.bft()
                sc.op("dve", lambda e, vo=vo, pb=pb: e.tensor_copy(out=vo, in_=pb[:, 0:512]), reads=[self.b_ps[tb]], writes=[bvo])
                pending.append(lambda g=g, vo=vo, bvo=bvo, tt=tt: sc.dma("sp", self.vs[g][:, tt * 4:(tt + 1) * 4, :],
                                                                         vo.rearrange("p (k n) -> p k n", k=4), bvo,
                                                                         reads=[bvo], writes=[self.b_vs[g]]))
            for fn in pending:
                fn()

    def p23(self, l, q, xsrc, xsrc_b):
        sc = self.sc
        for tt in range(NT):
            self.load_x_tile(xsrc[q], [xsrc_b[q][tt]], tt)
            self.load_cs(tt)
            self.norm_tile(O_GMIX + l * NCH)
            self.sgu(l)
            self.dbg("st", self.B1, [self.b_b1])
            self.dbg("vn", self.B3, [self.b_b3])
            self.branch(l, "w_sg_o", B_GS, self.B1_3, self.b_b1, first=True)
            self.dbg("sb", self.B2, [self.b_b2])
            self.attention(l)
            self.dbg("at", self.B1, [self.b_b1])
            self.branch(l, "w_attn_o", B_GA, self.B1_3, self.b_b1, first=False)
            self.dbg("mix", self.B2, [self.b_b2])
            self.lin_moving_residual("w_out", l, self.B2_3, self.b_b2, 4)
            self.dbg("x1", self.XT, [self.b_xt])
            self.cross(l)
            self.dbg("x2", self.XT, [self.b_xt])
            self.store_x_tile(self.xa[q], [self.b_xa[q][tt]], tt)

    def sgu(self, l):
        sc = self.sc
        UT3 = self.B1_3
        for m in range(NCH):
            bank = m % 2
            self.lin_ws("w_in", l, B_U + m, bank, self.HT3, self.b_ht)
            sc.op("act", lambda e, m=m, bank=bank: e.activation(out=UT3[:, m, :], in_=self.ps[bank][:, :], func=AF.Gelu_apprx_tanh),
                  reads=[self.b_ps[bank]], writes=[self.b_b1])
        for j in range(4):
            base = 4 if j % 2 == 0 else 0
            if j % 2 == 1:
                base = 2
            banks = (4, 5, 6, 7) if j % 2 == 0 else (2, 3, 6, 7)
            banks = (4, 5, 6, 7)
            for g4 in range(4):
                t, b = self.load_slot(self.wm["w_sv"][l, j, g4])
                t3 = t.rearrange("p (c n) -> p c n", c=4)
                for s_ in range(4):
                    def body(e, t3=t3, s_=s_, g4=g4, banks=banks):
                        r = None
                        for c in range(4):
                            r = e.matmul(self.ps[banks[s_]][:, :], lhsT=self.HT3[:, g4 * 4 + c, s_ * 128:(s_ + 1) * 128],
                                         rhs=t3[:, c, :], start=(g4 == 0 and c == 0), stop=(g4 == 3 and c == 3))
                        return r
                    sc.op("pe", body, reads=[b, self.b_ht], writes=[self.b_ps[banks[s_]]])
            for s_ in range(4):
                sc.op("act", lambda e, s_=s_, j=j, banks=banks: e.activation(out=self.GV3[:, s_, j * 512:(j + 1) * 512],
                                                                          in_=self.ps[banks[s_]][:, :], func=AF.Gelu_apprx_tanh),
                      reads=[self.b_ps[banks[s_]]], writes=[self.b_b3])
        S1, S2, MEAN, VAR, RSTD, NMR = [self.SM[:, 4 + i:5 + i] for i in range(6)]
        bs1, bs2, bmean, bvar, brstd, bnmr = [self.b_sm[4 + i] for i in range(6)]
        for s_ in range(4):
            gv = self.GV3[:, s_, :]
            sc.op("act", lambda e, gv=gv: e.activation(out=self.HB, in_=gv, func=AF.Identity, accum_out=S1),
                  reads=[self.b_b3], writes=[self.b_hb, bs1])
            sc.op("act", lambda e, gv=gv: e.activation(out=self.HB, in_=gv, func=AF.Square, accum_out=S2),
                  reads=[self.b_b3], writes=[self.b_hb, bs2])
            sc.op("dve", lambda e: e.tensor_scalar(out=MEAN, in0=S1, scalar1=1.0 / D, scalar2=None, op0=ALU.mult),
                  reads=[bs1], writes=[bmean])
            sc.op("dve", lambda e: e.tensor_tensor(out=VAR, in0=MEAN, in1=MEAN, op=ALU.mult), reads=[bmean], writes=[bvar])
            sc.op("dve", lambda e: e.scalar_tensor_tensor(out=VAR, in0=S2, scalar=1.0 / D, in1=VAR, op0=ALU.mult, op1=ALU.subtract),
                  reads=[bs2, bvar], writes=[bvar])
            sc.op("act", lambda e: e.activation(out=RSTD, in_=VAR, func=AF.Ln, bias=EPS), reads=[bvar], writes=[brstd])
            sc.op("act", lambda e: e.activation(out=RSTD, in_=RSTD, func=AF.Exp, scale=-0.5), reads=[brstd], writes=[brstd])
            sc.op("dve", lambda e: e.scalar_tensor_tensor(out=NMR, in0=MEAN, scalar=-1.0, in1=RSTD, op0=ALU.mult, op1=ALU.mult),
                  reads=[bmean, brstd], writes=[bnmr])
            sc.op("dve", lambda e, gv=gv: e.tensor_scalar(out=gv, in0=gv, scalar1=RSTD, scalar2=NMR, op0=ALU.mult, op1=ALU.add),
                  reads=[self.b_b3, brstd, bnmr], writes=[self.b_b3])
        for cb in range(NCH):
            g = cb // 2
            bank = cb % 2

            def body(e, cb=cb, g=g, bank=bank):
                r = None
                for s_ in range(4):
                    r = e.matmul(self.ps[bank][:, s_ * 128:(s_ + 1) * 128], lhsT=self.GV3[:, s_, cb * 128:(cb + 1) * 128],
                                 rhs=self.WST[:, g * 128:(g + 1) * 128], start=True, stop=True)
                return r
            sc.op("pe", body, reads=[self.b_b3, self.b_wst], writes=[self.b_ps[bank]])
            tb, btb = self.f32t()
            sc.op("dve", lambda e, tb=tb, g=g, cb=cb: e.scalar_tensor_tensor(
                out=tb[:, 0:128], in0=self.RSS[:, g * 128:(g + 1) * 128], scalar=self.col(O_SGB + l * NCH + cb),
                in1=self.BSB[:, g * 128:(g + 1) * 128], op0=ALU.mult, op1=ALU.add),
                reads=[self.b_rss, self.b_bsb, self.b_colp], writes=[btb])
            tm, btm = self.f32t()
            for s_ in range(4):
                sc.op("dve", lambda e, tm=tm, tb=tb, s_=s_, cb=cb, bank=bank: e.scalar_tensor_tensor(
                    out=tm[:, s_ * 128:(s_ + 1) * 128], in0=self.ps[bank][:, s_ * 128:(s_ + 1) * 128],
                    scalar=self.col(O_SGG + l * NCH + cb), in1=tb[:, 0:128], op0=ALU.mult, op1=ALU.add),
                    reads=[self.b_ps[bank], btb, self.b_colp], writes=[btm])
            sc.op("pool", lambda e, tm=tm, cb=cb: e.tensor_tensor(out=UT3[:, cb, :], in0=tm, in1=UT3[:, cb, :], op=ALU.mult),
                  reads=[btm, self.b_b1], writes=[self.b_b1])

    def branch(self, l, wname, gate_blk, in3, in_b, first):
        sc = self.sc
        for m in range(NCH):
            ba, bg = (0, 2) if m % 2 == 0 else (1, 3)
            self.lin_ws(wname, l, m, ba, in3, in_b)
            self.lin_ws("w_in", l, gate_blk + m, bg, self.HT3, self.b_ht)
            sg, bsg = self.f32t()
            sc.op("act", lambda e, sg=sg, bg=bg: e.activation(out=sg, in_=self.ps[bg][:, :], func=AF.Sigmoid),
                  reads=[self.b_ps[bg]], writes=[bsg])
            if first:
                sc.op("dve", lambda e, sg=sg, ba=ba, m=m: e.tensor_tensor(out=self.B2_3[:, m, :], in0=self.ps[ba][:, :], in1=sg, op=ALU.mult),
                      reads=[self.b_ps[ba], bsg], writes=[self.b_b2])
            else:
                tm, btm = self.f32t()
                sc.op("dve", lambda e, sg=sg, ba=ba, tm=tm: e.tensor_tensor(out=tm, in0=self.ps[ba][:, :], in1=sg, op=ALU.mult),
                      reads=[self.b_ps[ba], bsg], writes=[btm])
                sc.op("pool", lambda e, tm=tm, m=m: e.tensor_tensor(out=self.B2_3[:, m, :], in0=tm, in1=self.B2_3[:, m, :], op=ALU.add),
                      reads=[btm, self.b_b2], writes=[self.b_b2])

    def attention(self, l):
        sc = self.sc
        scale = float(HD) ** -0.5

        def banks_of(h):
            return (4, 6) if h % 2 == 0 else (5, 7)

        def prep(h):
            pbank = h % 2
            self.lin_ws("w_in", l, B_Q + h, pbank, self.HT3, self.b_ht)
            qt, bqt = self.QT[h % 2], self.b_qt[h % 2]
            self.qk_rope(pbank, banks_of(h)[1], self.col(O_QG + l), qt, bqt)
            self.dbg("qt%d" % h, qt, [bqt])
            return qt, bqt

        nxt = prep(0)
        kv = None
        for h in range(NH):
            g = h // 4
            if h % 4 == 0:
                ktt, bktt = self.load_slot(self.kts[g], reads=[self.b_kts[g]])
                vtt, bvtt = self.load_slot(self.vs[g].rearrange("p k n -> p (k n)"), reads=[self.b_vs[g]])
                v3 = vtt.rearrange("p (k n) -> p k n", k=NCH)
            qt, bqt = nxt
            if h + 1 < NH:
                nxt = prep(h + 1)
            ob, sb = banks_of(h)

            def s_mm(kt, ktt=ktt, bktt=bktt, qt=qt, bqt=bqt):
                bank = 2 + kt % 2
                sc.op("pe", lambda e, kt=kt, bank=bank, ktt=ktt, qt=qt: e.matmul(self.ps[bank][:, :], lhsT=ktt[:, kt * 128:(kt + 1) * 128], rhs=qt,
                                                                                 start=True, stop=True),
                      reads=[bktt, bqt], writes=[self.b_ps[bank]])
            s_mm(0)
            for kt in range(NCH):
                if kt + 1 < NCH:
                    s_mm(kt + 1)
                bank = 2 + kt % 2
                pt, bpt = self.bft()
                sc.op("act", lambda e, pt=pt, bank=bank: e.activation(out=pt, in_=self.ps[bank][:, :], func=AF.Exp, scale=scale),
                      reads=[self.b_ps[bank]], writes=[bpt])

                def body(e, kt=kt, pt=pt, ob=ob, sb=sb, v3=v3):
                    e.matmul(self.ps[ob][:, :], lhsT=v3[:, kt, :], rhs=pt, start=(kt == 0), stop=(kt == NCH - 1))
                    return e.matmul(self.ps[sb][:, :], lhsT=self.ONES, rhs=pt, start=(kt == 0), stop=(kt == NCH - 1))
                sc.op("pe", body, reads=[bvtt, bpt, self.b_ones], writes=[self.b_ps[ob], self.b_ps[sb]])
            rc, brc = self.f32t()
            sc.op("act", lambda e, rc=rc, sb=sb: e.activation(out=rc, in_=self.ps[sb][:, :], func=AF.Ln), reads=[self.b_ps[sb]], writes=[brc])
            sc.op("act", lambda e, rc=rc: e.activation(out=rc, in_=rc, func=AF.Exp, scale=-1.0), reads=[brc], writes=[brc])
            sc.op("dve", lambda e, rc=rc, ob=ob, h=h: e.tensor_tensor(out=self.B1_3[:, h, :], in0=self.ps[ob][:, :], in1=rc, op=ALU.mult),
                  reads=[self.b_ps[ob], brc], writes=[self.b_b1])

    def cross(self, l):
        sc = self.sc
        scale = 512.0 ** -0.5
        self.norm_tile(O_GX + l * NCH)
        QX3, OX3 = self.B1_3, self.B2_3
        for m in range(NCH):
            bank = m % 2
            self.lin_ws("w_xq", l, m, bank, self.HT3, self.b_ht)
            sc.op("act", lambda e, m=m, bank=bank: e.activation(out=QX3[:, m, :], in_=self.ps[bank][:, :], func=AF.Copy),
                  reads=[self.b_ps[bank]], writes=[self.b_b1])
        for hx in range(4):
            t, b = self.slot()
            sc.dma("sp", t[:, 0:1024], self.kms[hx], b, reads=[self.b_kms[hx]], writes=[b])
            sc.dma("sp", t[:, 1024:2048], self.vms[hx], b, reads=[self.b_vms[hx]], writes=[b])
            km3 = t[:, 0:1024].rearrange("p (m n) -> p m n", m=4)
            vm3 = t[:, 1024:2048].rearrange("p (k n) -> p k n", k=2)
            pts = []
            for kt in range(2):
                bank = kt

                def body(e, kt=kt, bank=bank, hx=hx, km3=km3):
                    r = None
                    for dc in range(4):
                        r = e.matmul(self.ps[bank][:, :], lhsT=km3[:, dc, kt * 128:(kt + 1) * 128], rhs=QX3[:, hx * 4 + dc, :],
                                     start=(dc == 0), stop=(dc == 3))
                    return r
                sc.op("pe", body, reads=[b, self.b_b1], writes=[self.b_ps[bank]])
                pt, bpt = self.bft()
                sc.op("act", lambda e, pt=pt, bank=bank: e.activation(out=pt, in_=self.ps[bank][:, :], func=AF.Exp, scale=scale),
                      reads=[self.b_ps[bank]], writes=[bpt])
                pts.append((pt, bpt))

            def body_sum(e, pts=pts):
                e.matmul(self.ps[2][:, :], lhsT=self.ONES, rhs=pts[0][0], start=True, stop=False)
                return e.matmul(self.ps[2][:, :], lhsT=self.ONES, rhs=pts[1][0], start=False, stop=True)
            sc.op("pe", body_sum, reads=[pts[0][1], pts[1][1], self.b_ones], writes=[self.b_ps[2]])
            rc, brc = self.f32t()
            sc.op("act", lambda e, rc=rc: e.activation(out=rc, in_=self.ps[2][:, :], func=AF.Ln), reads=[self.b_ps[2]], writes=[brc])
            sc.op("act", lambda e, rc=rc: e.activation(out=rc, in_=rc, func=AF.Exp, scale=-1.0), reads=[brc], writes=[brc])
            for dc in range(4):
                ob = 4 + dc

                def body_o(e, dc=dc, ob=ob, vm3=vm3, pts=pts):
                    e.matmul(self.ps[ob][:, :], lhsT=vm3[:, 0, dc * 128:(dc + 1) * 128], rhs=pts[0][0], start=True, stop=False)
                    return e.matmul(self.ps[ob][:, :], lhsT=vm3[:, 1, dc * 128:(dc + 1) * 128], rhs=pts[1][0], start=False, stop=True)
                sc.op("pe", body_o, reads=[b, pts[0][1], pts[1][1]], writes=[self.b_ps[ob]])
                sc.op("dve", lambda e, rc=rc, ob=ob, hx=hx, dc=dc: e.tensor_tensor(out=OX3[:, hx * 4 + dc, :], in0=self.ps[ob][:, :], in1=rc, op=ALU.mult),
                      reads=[self.b_ps[ob], brc], writes=[self.b_b2])
        self.dbg("qx", self.B1, [self.b_b1])
        self.dbg("ox", self.B2, [self.b_b2])
        self.lin_moving_residual("w_xo", l, OX3, self.b_b2, 4)

    def p4(self, l, q):
        sc = self.sc
        for tt in range(NT):
            self.load_x_tile(self.xa[q], [self.b_xa[q][tt]], tt)
            sc.op("pool", lambda e: e.memset(self.HALO, 0.0), writes=[self.b_halo])
            if tt > 0:
                r = tt * TT - 1
                sc.dma("sp", self.HALO[0:1, :], self.xa[q][r:r + 1, :], self.b_halo, reads=[self.b_xa[q][tt - 1]], writes=[self.b_halo])
            if tt < NT - 1:
                r = (tt + 1) * TT
                sc.dma("sp", self.HALO[1:2, :], self.xa[q][r:r + 1, :], self.b_halo, reads=[self.b_xa[q][tt + 1]], writes=[self.b_halo])
            goff = O_GFFN + l * NCH
            self.norm_tile(goff)
            self.norm_sub(self.HALO, self.b_halo, goff, lambda c: self.HTH3[:, c, :], self.b_hth, 2)
            for f in range(NF):
                pa, pb_, ph = (0, 2, 4) if f % 2 == 0 else (1, 3, 5)
                ta, ba = self.load_slot(self.ws["w_ffn_up"][l, f])
                ta3 = ta.rearrange("p (c n) -> p c n", c=NCH)

                def body(e, ta3=ta3, pa=pa, ph=ph):
                    r = None
                    for c in range(NCH):
                        e.matmul(self.ps[pa][:, :], lhsT=ta3[:, c, :], rhs=self.HT3[:, c, :], start=(c == 0), stop=(c == NCH - 1))
                        r = e.matmul(self.ps[ph][:, 0:2], lhsT=ta3[:, c, :], rhs=self.HTH3[:, c, :], start=(c == 0), stop=(c == NCH - 1))
                    return r
                sc.op("pe", body, reads=[ba, self.b_ht, self.b_hth], writes=[self.b_ps[pa], self.b_ps[ph]])
                self.lin_ws("w_ffn_up", l, NF + f, pb_, self.HT3, self.b_ht)
                ab, bab = self.ABUF[f % 2], self.b_abuf[f % 2]
                sc.op("act", lambda e, ab=ab, pa=pa: e.activation(out=ab[:, 1:513], in_=self.ps[pa][:, :], func=AF.Copy),
                      reads=[self.b_ps[pa]], writes=[bab])
                sc.op("act", lambda e, ab=ab, ph=ph: e.activation(out=ab[:, 0:1], in_=self.ps[ph][:, 0:1], func=AF.Copy),
                      reads=[self.b_ps[ph]], writes=[bab])
                sc.op("act", lambda e, ab=ab, ph=ph: e.activation(out=ab[:, 513:514], in_=self.ps[ph][:, 1:2], func=AF.Copy),
                      reads=[self.b_ps[ph]], writes=[bab])
                cw = O_CW + l * 3 * NF + f
                c1, bc1 = self.f32t()
                sc.op("pool", lambda e, c1=c1, ab=ab, cw=cw: e.tensor_scalar(out=c1, in0=ab[:, 0:512], scalar1=self.col(cw), scalar2=None, op0=ALU.mult),
                      reads=[bab, self.b_colp], writes=[bc1])
                sc.op("dve", lambda e, c1=c1, ab=ab, cw=cw: e.scalar_tensor_tensor(out=c1, in0=ab[:, 1:513], scalar=self.col(cw + NF), in1=c1,
                                                                                 op0=ALU.mult, op1=ALU.add),
                      reads=[bab, bc1, self.b_colp], writes=[bc1])
                sc.op("dve", lambda e, c1=c1, ab=ab, cw=cw: e.scalar_tensor_tensor(out=c1, in0=ab[:, 2:514], scalar=self.col(cw + 2 * NF), in1=c1,
                                                                                 op0=ALU.mult, op1=ALU.add),
                      reads=[bab, bc1, self.b_colp], writes=[bc1])
                sc.op("act", lambda e, c1=c1, f=f: e.activation(out=c1, in_=c1, func=AF.Gelu_apprx_tanh, bias=self.col(O_CB + l * NF + f)),
                      reads=[bc1, self.b_colp], writes=[bc1])
                sc.op("dve", lambda e, c1=c1, f=f, pb_=pb_: e.tensor_tensor(out=self.G3[:, f, :], in0=self.ps[pb_][:, :], in1=c1, op=ALU.mult),
                      reads=[self.b_ps[pb_], bc1], writes=[self.b_g])
            self.dbg("g", self.G, [self.b_g])
            self.lin_moving_residual("w_ffn_down", l, self.G3, self.b_g, 11)
            self.dbg("x3", self.XT, [self.b_xt])
            self.store_x_tile(self.xb[q], [self.b_xb[q][tt]], tt)

    def final(self, q, ydst):
        sc = self.sc
        sc.dma("sp", self.GBC, self.fng_in.partition_broadcast(128), self.b_gbc, writes=[self.b_gbc])
        SS, RT, RS = self.SM[:, 0:1], self.SM[:, 1:2], self.SM[:, 2:3]
        bss, brt, brs = self.b_sm[0], self.b_sm[1], self.b_sm[2]
        evs = []
        for tt in range(NT):
            self.load_x_tile(self.xb[q], [self.b_xb[q][tt]], tt)
            for s_ in range(4):
                xap = self.XT3[:, s_, :]
                sc.op("act", lambda e, xap=xap: e.activation(out=self.HB, in_=xap, func=AF.Square, accum_out=SS),
                      reads=[self.b_xt], writes=[self.b_hb, bss])
                sc.op("act", lambda e: e.activation(out=RT, in_=SS, func=AF.Ln, scale=1.0 / D, bias=EPS), reads=[bss], writes=[brt])
                sc.op("act", lambda e: e.activation(out=RS, in_=RT, func=AF.Exp, scale=-0.5), reads=[brt], writes=[brs])
                sc.op("dve", lambda e, xap=xap: e.scalar_tensor_tensor(out=xap, in0=xap, scalar=RS, in1=self.GBC, op0=ALU.mult, op1=ALU.mult),
                      reads=[self.b_xt, brs, self.b_gbc], writes=[self.b_xt])
            evs.append(self.store_x_tile(ydst, [], tt))
        return evs


def build(nseq, nlayer, do_prologue=True, debug=False, stop=None):
    nc = bass.Bass("TRN2", target_bir_lowering=False)
    st = ExitStack()
    mk = MK(nc, st, nseq, nlayer)
    mk.debug = debug
    mk.stop = stop
    sc = mk.sc
    if do_prologue:
        mk.prologue()
        sc.run()
        sc.clear_sems()
        sc.reset()
    with nc.Fori(0, nseq) as qv:
        mk.setup_main()
        b_cp = mk.dbuf("cpin")
        xsrc = mk.x_in[bass.ds(qv, 1)].rearrange("a s n -> (a s) n")
        msrc = mk.mem_in[bass.ds(qv, 1)].rearrange("a s n -> (a s) n")
        ydst = mk.y_out[bass.ds(qv, 1)].rearrange("a s n -> (a s) n")
        for tt in range(NT):
            sc.dma("sp", mk.xb[0][tt * TT:(tt + 1) * TT, :].rearrange("(s p) n -> p s n", p=128),
                   xsrc[tt * TT:(tt + 1) * TT, :].rearrange("(s p) n -> p s n", p=128), b_cp, writes=[mk.b_xb[0][tt]])
        sc.dma("sp", mk.memc.rearrange("(s p) n -> p s n", p=128), msrc.rearrange("(s p) n -> p s n", p=128), b_cp,
               writes=[mk.b_memc])
        try:
            for l in range(nlayer):
                mk.layer_consts(l)
                mk.mem_kv(l, 0)
                mk.p1(l, 0, mk.xb, mk.b_xb)
                mk.p23(l, 0, mk.xb, mk.b_xb)
                mk.p4(l, 0)
            mk.final(0, ydst)
        except StopBuild:
            pass
        mk.barrier()
        sc.run()
        sc.clear_sems()
    st.close()
    return nc, mk


def host_consts(inp):
    L = NL

    def colmajor(a):
        a = np.asarray(a, np.float32)
        c = a.shape[1] // 128
        return np.ascontiguousarray(a.reshape(L, c, 128).transpose(2, 0, 1).reshape(128, L * c))

    colp = np.zeros((128, NCOL), np.float32)
    for name, off in (("norm_mix_g", O_GMIX), ("norm_x_g", O_GX), ("norm_mem_g", O_GMEM), ("norm_ffn_g", O_GFFN),
                      ("sg_norm_g", O_SGG), ("sg_norm_b", O_SGB)):
        colp[:, off:off + L * NCH] = colmajor(inp[name])
    colp[:, O_QG:O_QG + L] = np.asarray(inp["q_norm_g"], np.float32).T
    colp[:, O_KG:O_KG + L] = np.asarray(inp["k_norm_g"], np.float32).T
    cw = np.asarray(inp["conv_w"], np.float32).reshape(L, 3, NF, 128).transpose(3, 0, 1, 2).reshape(128, L * 3 * NF)
    colp[:, O_CW:O_CW + L * 3 * NF] = cw
    colp[:, O_CB:O_CB + L * NF] = colmajor(inp["conv_b"])
    wst = np.ascontiguousarray(np.asarray(inp["w_spatial"], np.float32).transpose(0, 3, 1, 2).reshape(L, 128, 1024))
    bsp = np.ascontiguousarray(np.asarray(inp["b_spatial"], np.float32).reshape(L, 1, 1024))
    fng = np.ascontiguousarray(np.asarray(inp["final_norm_g"], np.float32).reshape(1, D))
    t = np.arange(S)
    row = (t // 64).astype(np.float32)
    colv = (t % 64).astype(np.float32)
    quarter = HD // 4
    inv = (np.float32(10000.0) ** (-np.arange(quarter, dtype=np.float32) / np.float32(quarter))).astype(np.float32)
    ang_r = row[:, None] * inv[None, :]
    ang_c = colv[:, None] * inv[None, :]
    ang = np.concatenate([ang_r, ang_r, ang_c, ang_c], axis=-1).astype(np.float32)
    cosT = np.ascontiguousarray(np.cos(ang).T.astype(np.float32))
    sinT = np.sin(ang).T.astype(np.float32)
    sign = np.ones((128, 1), np.float32)
    sign[32:64] = -1.0
    sign[96:128] = -1.0
    sinP = np.ascontiguousarray(sinT * sign)
    return dict(colp=colp, wst=wst, bsp=bsp, fng=fng, cosT=cosT, sinP=sinP, ident=np.eye(128, dtype=np.float32))


W_NAMES = ("w_in", "w_attn_o", "w_sg_o", "w_out", "w_xq", "w_xkv", "w_xo", "w_ffn_up", "w_ffn_down")
_CACHE = {}


def kernel(**inp):
    xs = [np.asarray(inp["x_prompt"][i], np.float32) for i in range(4)] + [np.asarray(inp["x_sample"][i], np.float32) for i in range(16)]
    ms = [np.asarray(inp["mem_prompt"][i], np.float32) for i in range(4)] + [np.asarray(inp["mem_sample"][i], np.float32) for i in range(16)]
    nseq = SEQ_PER_CORE
    if "nc" not in _CACHE:
        _CACHE["nc"] = build(nseq, NL)[0]
    nc = _CACHE["nc"]
    consts = host_consts(inp)
    weights = {n: np.ascontiguousarray(np.asarray(inp[n], np.float32)) for n in W_NAMES}
    in_maps = []
    slots = []
    for c in range(NCORES):
        ids = [c + NCORES * j for j in range(nseq)]
        slots.append(ids)
        x = np.zeros((nseq, S, D), np.float32)
        m = np.zeros((nseq, NMEM, D), np.float32)
        for j, i in enumerate(ids):
            if i < len(xs):
                x[j] = xs[i]
                m[j] = ms[i]
        d = {"x": x, "mem": m}
        d.update(weights)
        d.update(consts)
        in_maps.append(d)
    res = run_bass_kernel_spmd(nc, in_maps, core_ids=list(range(NCORES)))
    ys = [None] * 20
    for c in range(NCORES):
        y = res.results[c]["y"]
        for j, i in enumerate(slots[c]):
            if i < 20:
                ys[i] = np.asarray(y[j], np.float32)
    y_prompt = np.stack(ys[0:4], axis=0)
    y_sample = np.stack(ys[4:20], axis=0)
    return (y_prompt, y_sample)
```
